# Optimizing a Trainium2 kernel written in Bass

```python
import math
import jax, jax.numpy as jnp
from jax import lax
import numpy as np

D_MODEL = 2048
BATCH = 4
SEQ = 4096
DEPTH = 4
DEC_BATCH = 8
DEC_SEQ = 4096
PAST_LEN = 128

N_META = 16
MLA_HEADS = 16
MLA_Q_LORA = 512
MLA_KV_LORA = 512
MLA_NOPE = 128
MLA_ROPE = 64
MLA_V = 128
ROPE_THETA = 10000.0
SWA_HEADS = 32
SWA_KV_HEADS = 4
SWA_GROUP = SWA_HEADS // SWA_KV_HEADS
SWA_HEAD_DIM = 64
WINDOW = 128
BLOCK = 128
N_BUCKETS = 32
MAX_DISTANCE = 128
D_FF = 4 * D_MODEL
N_MIXERS = 2
N_MLA_LAYERS = (DEPTH + 1) // 2
N_SWA_LAYERS = DEPTH // 2
EPS = 1e-6

kernel_name = "hybrid_mla_swa_sink_encoder"


def _rmsnorm(x, g):
    xf = x.astype(jnp.float32)
    r = lax.rsqrt(jnp.mean(xf * xf, axis=-1, keepdims=True) + EPS)
    return (xf * r * g.astype(jnp.float32)).astype(x.dtype)


def _rope(x, pos):
    half = x.shape[-1] // 2
    inv = ROPE_THETA ** (-(jnp.arange(half, dtype=jnp.float32) / half))
    ang = pos.astype(jnp.float32)[:, None] * inv[None, :]
    cos = jnp.cos(ang)[None, :, None, :]
    sin = jnp.sin(ang)[None, :, None, :]
    x1 = x[..., :half].astype(jnp.float32)
    x2 = x[..., half:].astype(jnp.float32)
    return jnp.concatenate([x1 * cos - x2 * sin, x2 * cos + x1 * sin], axis=-1).astype(x.dtype)


def _t5_bucket(rel):
    nb = N_BUCKETS // 2
    max_exact = nb // 2
    ret = jnp.where(rel > 0, nb, 0)
    n = jnp.abs(rel)
    nf = jnp.maximum(n, 1).astype(jnp.float32)
    large = max_exact + (jnp.log(nf / max_exact) / math.log(MAX_DISTANCE / max_exact)
                         * (nb - max_exact)).astype(jnp.int32)
    large = jnp.minimum(large, nb - 1)
    return ret + jnp.where(n < max_exact, n, large)


def _swa_attend(q, k, v, qpos, kpos, valid, rel_bias, sink):
    lq, lk = q.shape[1], k.shape[1]
    s = jnp.einsum('bqkgd,bskd->bkgqs', q, k).astype(jnp.float32) * (SWA_HEAD_DIM ** -0.5)
    bias = rel_bias[_t5_bucket(kpos[None, :] - qpos[:, None])].astype(jnp.float32)
    bias = bias.reshape(lq, lk, SWA_KV_HEADS, SWA_GROUP).transpose(2, 3, 0, 1)
    s = jnp.where(valid[None, None, None], s + bias[None], -jnp.inf)
    sk = sink.astype(jnp.float32).reshape(SWA_KV_HEADS, SWA_GROUP)[None, :, :, None, None]
    m = jnp.maximum(jnp.max(s, axis=-1, keepdims=True), sk)
    e = jnp.exp(s - m)
    p = e / (jnp.sum(e, axis=-1, keepdims=True) + jnp.exp(sk - m))
    return jnp.einsum('bkgqs,bskd->bqkgd', p.astype(v.dtype), v)


def _swa_mixer(x, w_qkv, w_o, sink, rel_bias):
    B, L, _ = x.shape
    S = L - N_META
    nb = S // BLOCK
    dq = SWA_HEADS * SWA_HEAD_DIM
    dk = SWA_KV_HEADS * SWA_HEAD_DIM
    qkv = x @ w_qkv
    q = qkv[..., :dq].reshape(B, L, SWA_KV_HEADS, SWA_GROUP, SWA_HEAD_DIM)
    k = qkv[..., dq:dq + dk].reshape(B, L, SWA_KV_HEADS, SWA_HEAD_DIM)
    v = qkv[..., dq + dk:].reshape(B, L, SWA_KV_HEADS, SWA_HEAD_DIM)
    n_lead = N_META + BLOCK
    pos_lead = jnp.arange(n_lead)
    qpos_m = jnp.arange(N_META)
    valid_m = (pos_lead[None, :] < N_META) | (jnp.abs(pos_lead[None, :] - qpos_m[:, None]) <= WINDOW)
    o_meta = _swa_attend(q[:, :N_META], k[:, :n_lead], v[:, :n_lead], qpos_m, pos_lead, valid_m,
                         rel_bias, sink)
    pad = ((0, 0), (BLOCK, BLOCK), (0, 0), (0, 0))
    k_pad = jnp.pad(k[:, N_META:], pad)
    v_pad = jnp.pad(v[:, N_META:], pad)
    k_meta, v_meta = k[:, :N_META], v[:, :N_META]
    q_blocks = jnp.moveaxis(
        q[:, N_META:].reshape(B, nb, BLOCK, SWA_KV_HEADS, SWA_GROUP, SWA_HEAD_DIM), 1, 0)
    offs = jnp.arange(BLOCK)
    band = jnp.arange(3 * BLOCK)
    meta_pos = jnp.arange(N_META)

    def step(args):
        qb, b = args
        start = b * BLOCK
        kb = jnp.concatenate([k_meta, lax.dynamic_slice_in_dim(k_pad, start, 3 * BLOCK, axis=1)], axis=1)
        vb = jnp.concatenate([v_meta, lax.dynamic_slice_in_dim(v_pad, start, 3 * BLOCK, axis=1)], axis=1)
        qpos = N_META + start + offs
        ridx = start - BLOCK + band
        kpos = jnp.concatenate([meta_pos, N_META + ridx])
        in_band = ((ridx[None, :] >= 0) & (ridx[None, :] < S)
                   & (jnp.abs(N_META + ridx[None, :] - qpos[:, None]) <= WINDOW))
        valid = jnp.concatenate([jnp.ones((BLOCK, N_META), dtype=bool), in_band], axis=1)
        return _swa_attend(qb, kb, vb, qpos, kpos, valid, rel_bias, sink)

    o_real = lax.map(step, (q_blocks, jnp.arange(nb)))
    o_real = jnp.moveaxis(o_real, 0, 1).reshape(B, S, dq)
    o = jnp.concatenate([o_meta.reshape(B, N_META, dq), o_real], axis=1)
    return o @ w_o


def _mla_attend(q_nope, q_rope, k_nope, k_rope, v):
    s = (jnp.einsum('bqhd,bkhd->bhqk', q_nope, k_nope)
         + jnp.einsum('bqhr,bkr->bhqk', q_rope, k_rope)).astype(jnp.float32)
    p = jax.nn.softmax(s * ((MLA_NOPE + MLA_ROPE) ** -0.5), axis=-1)
    return jnp.einsum('bhqk,bkhd->bqhd', p.astype(v.dtype), v)


def _mla_mixer(x, w_dq, q_norm, w_uq, w_dkv, kv_norm, w_ukv, w_o):
    B, L, _ = x.shape
    S = L - N_META
    nb = S // BLOCK
    pos = jnp.arange(L)
    c_q = _rmsnorm(x @ w_dq, q_norm)
    q = (c_q @ w_uq).reshape(B, L, MLA_HEADS, MLA_NOPE + MLA_ROPE)
    q_nope = q[..., :MLA_NOPE]
    q_rope = _rope(q[..., MLA_NOPE:], pos)
    kv_a = x @ w_dkv
    c_kv = _rmsnorm(kv_a[..., :MLA_KV_LORA], kv_norm)
    k_rope = _rope(kv_a[..., None, MLA_KV_LORA:], pos)[:, :, 0]
    kv = (c_kv @ w_ukv).reshape(B, L, MLA_HEADS, MLA_NOPE + MLA_V)
    k_nope = kv[..., :MLA_NOPE]
    v = kv[..., MLA_NOPE:]
    o_meta = _mla_attend(q_nope[:, :N_META], q_rope[:, :N_META], k_nope, k_rope, v)

    def blocks(t):
        return jnp.moveaxis(t[:, N_META:].reshape(B, nb, BLOCK, *t.shape[2:]), 1, 0)

    o_real = lax.map(lambda a: _mla_attend(a[0], a[1], k_nope, k_rope, v),
                     (blocks(q_nope), blocks(q_rope)))
    o_real = jnp.moveaxis(o_real, 0, 1).reshape(B, S, MLA_HEADS * MLA_V)
    o = jnp.concatenate([o_meta.reshape(B, N_META, MLA_HEADS * MLA_V), o_real], axis=1)
    return o @ w_o


def _mlp(x, w_in, w_out):
    h = jax.nn.relu(x @ w_in)
    return (h * h) @ w_out


def _trunk(x, meta_tokens, rel_bias, mla_w_dq, mla_q_norm, mla_w_uq, mla_w_dkv, mla_kv_norm,
           mla_w_ukv, mla_w_o, swa_w_qkv, swa_w_o, swa_sink, mlp_w_in, mlp_w_out,
           norm_mix_pre, norm_mix_post, norm_mlp_pre, norm_mlp_post):
    B = x.shape[0]
    meta = jnp.broadcast_to(meta_tokens[None].astype(x.dtype), (B, N_META, x.shape[-1]))
    h = jnp.concatenate([meta, x], axis=1)
    for i in range(DEPTH):
        j = i // N_MIXERS
        hn = _rmsnorm(h, norm_mix_pre[i])
        if i % N_MIXERS == 0:
            m = _mla_mixer(hn, mla_w_dq[j], mla_q_norm[j], mla_w_uq[j], mla_w_dkv[j], mla_kv_norm[j],
                           mla_w_ukv[j], mla_w_o[j])
        else:
            m = _swa_mixer(hn, swa_w_qkv[j], swa_w_o[j], swa_sink[j], rel_bias)
        h = h + _rmsnorm(m, norm_mix_post[i])
        f = _mlp(_rmsnorm(h, norm_mlp_pre[i]), mlp_w_in[i], mlp_w_out[i])
        h = h + _rmsnorm(f, norm_mlp_post[i])
    return h[:, N_META:]


def _dense(k, shape, fan_in):
    return jax.random.normal(k, shape, dtype=jnp.float32) * (fan_in ** -0.5)


def _gain(k, shape):
    return 1.0 + 0.05 * jax.random.normal(k, shape, dtype=jnp.float32)


def setup_inputs(seed: int = 0) -> dict:
    key = jax.random.key(seed)
    ks = jax.random.split(key, 20)
    nm, ns = N_MLA_LAYERS, N_SWA_LAYERS
    dqkv = (SWA_HEADS + 2 * SWA_KV_HEADS) * SWA_HEAD_DIM
    return {
        "x_prompt": jax.random.normal(ks[0], (BATCH, SEQ, D_MODEL), dtype=jnp.float32),
        "x_sample": jax.random.normal(ks[1], (DEC_BATCH, DEC_SEQ, D_MODEL), dtype=jnp.float32),
        "meta_tokens": jax.random.normal(ks[2], (N_META, D_MODEL), dtype=jnp.float32),
        "rel_bias": 0.5 * jax.random.normal(ks[3], (N_BUCKETS, SWA_HEADS), dtype=jnp.float32),
        "mla_w_dq": _dense(ks[4], (nm, D_MODEL, MLA_Q_LORA), D_MODEL),
        "mla_q_norm": _gain(ks[5], (nm, MLA_Q_LORA)),
        "mla_w_uq": _dense(ks[6], (nm, MLA_Q_LORA, MLA_HEADS * (MLA_NOPE + MLA_ROPE)), MLA_Q_LORA),
        "mla_w_dkv": _dense(ks[7], (nm, D_MODEL, MLA_KV_LORA + MLA_ROPE), D_MODEL),
        "mla_kv_norm": _gain(ks[8], (nm, MLA_KV_LORA)),
        "mla_w_ukv": _dense(ks[9], (nm, MLA_KV_LORA, MLA_HEADS * (MLA_NOPE + MLA_V)), MLA_KV_LORA),
        "mla_w_o": _dense(ks[10], (nm, MLA_HEADS * MLA_V, D_MODEL), MLA_HEADS * MLA_V),
        "swa_w_qkv": _dense(ks[11], (ns, D_MODEL, dqkv), D_MODEL),
        "swa_w_o": _dense(ks[12], (ns, SWA_HEADS * SWA_HEAD_DIM, D_MODEL), SWA_HEADS * SWA_HEAD_DIM),
        "swa_sink": 0.5 * jax.random.normal(ks[13], (ns, SWA_HEADS), dtype=jnp.float32),
        "mlp_w_in": _dense(ks[14], (DEPTH, D_MODEL, D_FF), D_MODEL),
        "mlp_w_out": _dense(ks[15], (DEPTH, D_FF, D_MODEL), D_FF),
        "norm_mix_pre": _gain(ks[16], (DEPTH, D_MODEL)),
        "norm_mix_post": _gain(ks[17], (DEPTH, D_MODEL)),
        "norm_mlp_pre": _gain(ks[18], (DEPTH, D_MODEL)),
        "norm_mlp_post": _gain(ks[19], (DEPTH, D_MODEL)),
    }


def reference(x_prompt, x_sample, meta_tokens, rel_bias, mla_w_dq, mla_q_norm, mla_w_uq, mla_w_dkv,
              mla_kv_norm, mla_w_ukv, mla_w_o, swa_w_qkv, swa_w_o, swa_sink, mlp_w_in, mlp_w_out,
              norm_mix_pre, norm_mix_post, norm_mlp_pre, norm_mlp_post):
    weights = (meta_tokens, rel_bias, mla_w_dq, mla_q_norm, mla_w_uq, mla_w_dkv, mla_kv_norm,
               mla_w_ukv, mla_w_o, swa_w_qkv, swa_w_o, swa_sink, mlp_w_in, mlp_w_out,
               norm_mix_pre, norm_mix_post, norm_mlp_pre, norm_mlp_post)
    y_prompt = _trunk(x_prompt, *weights)
    y_sample = _trunk(x_sample, *weights)
    return (y_prompt, y_sample)
```

```python
import math
import contextlib
import numpy as np
import ml_dtypes
import concourse.bass as bass
import concourse.mybir as mybir
from concourse.ap import AP
from concourse.bass_utils import run_bass_kernel_spmd

F32 = mybir.dt.float32
BF16 = mybir.dt.bfloat16
AF = mybir.ActivationFunctionType
ALU = mybir.AluOpType

D = 2048
KC = 16
NMETA = 16
DFF = 8192
EPS = 1e-6
NCORES = 8


class Buf:
    __slots__ = ("name", "w", "r", "const")

    def __init__(self, name="", const=False):
        self.name = name
        self.w = None
        self.r = []
        self.const = const


class Op:
    __slots__ = ("eng", "fn", "deps", "sig", "sem", "val", "isdma", "idx")


class Prog:
    ENGS = ("pe", "act", "dve", "pool", "sp")

    def __init__(self):
        self.ops = []
        self.last = {}
        self.lastdma = {}

    def op(self, eng, fn, reads=(), writes=(), dma=None):
        o = Op()
        o.eng = eng
        o.fn = fn
        o.deps = set()
        o.sig = False
        o.sem = dma
        o.val = 0
        o.isdma = dma is not None
        o.idx = len(self.ops)
        deps = o.deps
        isdma = o.isdma
        for b in reads:
            if b.w is not None:
                deps.add(b.w)
        for b in writes:
            x = b.w
            if x is not None and (isdma or x.isdma or x.eng != eng):
                deps.add(x)
            for x in b.r:
                if isdma or x.isdma or x.eng != eng:
                    deps.add(x)
        for b in reads:
            if not b.const:
                if not isdma and b.r:
                    b.r = [x for x in b.r if x.isdma or x.eng != eng]
                b.r.append(o)
        for b in writes:
            b.w = o
            b.r = []
        self.ops.append(o)
        if o.isdma:
            self.lastdma[dma] = o
        else:
            self.last[eng] = o
        return o

    def barrier(self):
        lasts = list(self.last.values()) + list(self.lastdma.values())
        for e in self.ENGS:
            o = Op()
            o.eng = e
            o.fn = None
            o.deps = set(lasts)
            o.sig = False
            o.sem = None
            o.val = 0
            o.isdma = False
            o.idx = len(self.ops)
            self.ops.append(o)

    def finalize(self):
        for o in self.ops:
            nd = set()
            for d in o.deps:
                if d is o:
                    continue
                if (not d.isdma) and (not o.isdma) and d.eng == o.eng and o.eng == "pe":
                    continue
                nd.add(d)
            o.deps = nd
            for d in nd:
                d.sig = True
        cnt = {e: 0 for e in self.ENGS}
        dcnt = {}
        for o in self.ops:
            if o.isdma:
                dcnt[o.sem] = dcnt.get(o.sem, 0) + 16
                o.val = dcnt[o.sem]
            elif o.sig:
                cnt[o.eng] += 1
                o.val = cnt[o.eng]
        self.dma_keys = list(dcnt.keys())
        self.stats = (len(self.ops), dict(cnt), len(dcnt))

    def emit(self, nc):
        with contextlib.ExitStack() as st:
            sems = {}
            for e in self.ENGS:
                sems[e] = st.enter_context(nc.semaphore("s_" + e))
            for k in self.dma_keys:
                sems[("dma", k)] = st.enter_context(nc.semaphore("d_" + str(k)))
            block = st.enter_context(nc.Block())
            ops = self.ops

            def run(engname):
                def body(eng):
                    seen = {}
                    for o in ops:
                        if o.eng != engname:
                            continue
                        if o.deps:
                            for d in sorted(o.deps, key=lambda x: x.idx):
                                key = ("dma", d.sem) if d.isdma else d.eng
                                if seen.get(key, 0) >= d.val:
                                    continue
                                seen[key] = d.val
                                eng.wait_ge(sems[key], d.val)
                        if o.fn is None:
                            continue
                        ins = o.fn(eng)
                        if o.isdma:
                            ins.then_inc(sems[("dma", o.sem)], 16)
                        elif o.sig:
                            ins.then_inc(sems[o.eng], 1)
                    if engname == "sp":
                        for k, o in self.lastdma.items():
                            if seen.get(("dma", k), 0) < o.val:
                                eng.wait_ge(sems[("dma", k)], o.val)
                return body

            block.tensor(run("pe"))
            block.scalar(run("act"))
            block.vector(run("dve"))
            block.gpsimd(run("pool"))
            block.sync(run("sp"))


class Cfg:
    def __init__(self, S=4096, depth=4, nseq=2, dbg=False):
        self.S = S
        self.L = S + NMETA
        self.depth = depth
        self.nseq = nseq
        self.NB = S // 128
        self.NST = S // 512
        self.dbg = dbg
        assert S % 512 == 0
        self.mch = {}
        n = 0
        for i in range(depth):
            if i % 2 == 0:
                for nm, c in (("dq", 4), ("dkv", 5), ("wo", 16)):
                    self.mch[(i, nm)] = (n, c)
                    n += c
            else:
                for nm, c in (("kv", 4), ("q", 16), ("wo", 16)):
                    self.mch[(i, nm)] = (n, c)
                    n += c
            self.mch[(i, "win")] = (n, 64)
            n += 64
        self.NM = n
        self.nmla = (depth + 1) // 2
        self.nswa = depth // 2

    def segs(self, s):
        r = [(512 * s, 512)]
        if s == self.NST - 1:
            r.append((self.S, NMETA))
        return r

    def ktiles(self):
        r = [(128 * b, 128) for b in range(self.NB)]
        r.append((self.S, NMETA))
        return r


def t5_bucket_np(rel):
    nb = 16
    me = 8
    rel = np.asarray(rel)
    ret = np.where(rel > 0, nb, 0)
    n = np.abs(rel)
    nf = np.maximum(n, 1).astype(np.float32)
    large = me + (np.log(nf / np.float32(me)) / np.float32(math.log(128 / me)) * np.float32(nb - me)).astype(np.int32)
    large = np.minimum(large, nb - 1)
    return ret + np.where(n < me, n, large)


OH_MAIN = 0
OH_K0 = 512
OH_KC = 656
OH_Q4 = 800
OH_Q5 = 832
OH_N = 976


def build_onehot():
    oh = np.zeros((33, OH_N), np.float32)

    def fill(base, n, relf, validf):
        for j in range(n):
            rel = relf(j)
            if validf(rel):
                oh[int(t5_bucket_np(rel)), base + j] = 1.0
            else:
                oh[32, base + j] = 1.0

    fill(OH_MAIN, 512, lambda i: i - 255, lambda r: abs(r) <= 128)
    fill(OH_K0, 144, lambda i: i - 143, lambda r: True)
    fill(OH_KC, 144, lambda i: i - 143 - 128, lambda r: True)
    fill(OH_Q4, 32, lambda i: i - 15, lambda r: True)
    fill(OH_Q5, 144, lambda i: i + 1, lambda r: r <= 128)
    return oh


def build_cs(cfg):
    S, L = cfg.S, cfg.L
    pos = np.concatenate([np.arange(S) + NMETA, np.arange(NMETA)]).astype(np.float32)
    half = 32
    inv = (np.float32(10000.0) ** (-(np.arange(half, dtype=np.float32) / np.float32(half)))).astype(np.float32)
    ang = (pos[None, :] * inv[:, None]).astype(np.float32)
    c = np.cos(ang).astype(np.float32)
    s = np.sin(ang).astype(np.float32)
    return np.concatenate([c, c, -s, s], axis=0).astype(np.float32)


def mchunk(src, cols, rows=None):
    a = src[:, cols]
    if rows is not None:
        a = a[rows, :]
    return np.ascontiguousarray(a.reshape(KC, 128, 128).transpose(1, 0, 2))


def swa_q_cols(c):
    p, j = divmod(c, 8)
    h0 = 8 * (2 * p) + j
    h1 = 8 * (2 * p + 1) + j
    return np.concatenate([np.arange(h0 * 64, h0 * 64 + 64), np.arange(h1 * 64, h1 * 64 + 64)])


def prep_weights(cfg, w):
    W = np.empty((cfg.NM, 128, KC, 128), np.float32)
    for i in range(cfg.depth):
        j = i // 2
        if i % 2 == 0:
            m0, _ = cfg.mch[(i, "dq")]
            for k in range(4):
                W[m0 + k] = mchunk(w["mla_w_dq"][j], np.arange(128 * k, 128 * k + 128))
            m0, _ = cfg.mch[(i, "dkv")]
            for k in range(4):
                W[m0 + k] = mchunk(w["mla_w_dkv"][j], np.arange(128 * k, 128 * k + 128))
            rc = np.concatenate([np.arange(512, 576), np.arange(544, 576), np.arange(512, 544)])
            W[m0 + 4] = mchunk(w["mla_w_dkv"][j], rc)
            m0, _ = cfg.mch[(i, "wo")]
            for k in range(16):
                W[m0 + k] = mchunk(w["mla_w_o"][j], np.arange(128 * k, 128 * k + 128))
        else:
            m0, _ = cfg.mch[(i, "kv")]
            for k in range(4):
                W[m0 + k] = mchunk(w["swa_w_qkv"][j], np.arange(2048 + 128 * k, 2048 + 128 * k + 128))
            m0, _ = cfg.mch[(i, "q")]
            for k in range(16):
                W[m0 + k] = mchunk(w["swa_w_qkv"][j], swa_q_cols(k))
            m0, _ = cfg.mch[(i, "wo")]
            rows = np.concatenate([swa_q_cols(k) for k in range(16)])
            for k in range(16):
                W[m0 + k] = mchunk(w["swa_w_o"][j], np.arange(128 * k, 128 * k + 128), rows=rows)
        m0, _ = cfg.mch[(i, "win")]
        win = w["mlp_w_in"][i]
        W[m0:m0 + 64] = win.reshape(KC, 128, 64, 128).transpose(2, 1, 0, 3)
    wout = np.ascontiguousarray(
        np.asarray(w["mlp_w_out"][:cfg.depth]).reshape(cfg.depth, 64, 128, 16, 128).transpose(0, 3, 2, 1, 4))
    nm = cfg.nmla
    wuq = np.empty((nm, 16, 128, 4, 256), np.float32)
    wukv = np.empty((nm, 16, 128, 4, 256), np.float32)
    for j in range(nm):
        for h in range(16):
            b = h * 192
            cols = np.concatenate([np.arange(b, b + 192), np.arange(b + 160, b + 192), np.arange(b + 128, b + 160)])
            wuq[j, h] = w["mla_w_uq"][j][:, cols].reshape(4, 128, 256).transpose(1, 0, 2)
            wukv[j, h] = w["mla_w_ukv"][j][:, h * 256:(h + 1) * 256].reshape(4, 128, 256).transpose(1, 0, 2)
    cols = []
    for i in range(cfg.depth):
        for nmn in ("norm_mix_pre", "norm_mix_post", "norm_mlp_pre", "norm_mlp_post"):
            cols.append(np.asarray(w[nmn][i]).reshape(16, 128).T)
    for j in range(nm):
        cols.append(np.asarray(w["mla_q_norm"][j]).reshape(4, 128).T)
        cols.append(np.asarray(w["mla_kv_norm"][j]).reshape(4, 128).T)
    gains = np.ascontiguousarray(np.concatenate(cols, axis=1).astype(np.float32))
    return W, wout, wuq, wukv, gains


def gain_col(cfg, i, kind):
    return i * 64 + {"mix_pre": 0, "mix_post": 16, "mlp_pre": 32, "mlp_post": 48}[kind]


def lat_gain_col(cfg, j, kind):
    return cfg.depth * 64 + j * 8 + (0 if kind == "q" else 4)


class Arena:
    def __init__(self, nc, base, top):
        self.nc = nc
        self.base = base
        self.off = base
        self.top = top
        self.n = 0

    def alloc(self, shape, dt):
        nbytes = int(np.prod(shape[1:])) * (4 if dt == F32 else 2)
        nbytes = (nbytes + 63) // 64 * 64
        assert self.off + nbytes <= self.top, ("SBUF overflow", self.off, nbytes, self.top)
        self.n += 1
        h = self.nc.alloc_sbuf_tensor_at("t%d" % self.n, list(shape), dt, offset=self.off)
        self.off += nbytes
        return h

    def mark(self):
        return self.off

    def release(self, m):
        self.off = m


class Ring:
    def __init__(self, items):
        self.items = items
        self.i = 0

    def next(self):
        it = self.items[self.i % len(self.items)]
        self.i += 1
        return it


def build_program(cfg):
    nc = bass.Bass("TRN2", target_bir_lowering=False)
    S, L, NST, NB = cfg.S, cfg.L, cfg.NST, cfg.NB
    NSEQ = cfg.nseq
    P = Prog()

    def din(name, shape, dt=F32):
        return nc.dram_tensor(name, list(shape), dt, kind="ExternalInput").ap()

    def dscr(name, shape, dt):
        return nc.dram_tensor(name, list(shape), dt, kind="Internal").ap()

    xT = din("xT", [NSEQ, D, S])
    metaT = din("metaT", [D, NMETA])
    Wf = din("W2048", [cfg.NM, 128, KC * 128])
    Woutf = din("Wout", [cfg.depth * 16, 128, 64 * 128])
    Wuqf = din("Wuq", [cfg.nmla * 16, 128, 1024])
    Wukvf = din("Wukv", [cfg.nmla * 16, 128, 1024])
    gains_d = din("gains", [128, cfg.depth * 64 + cfg.nmla * 8])
    relb_d = din("rel_bias", [32, 32])
    sink_d = din("sink", [max(cfg.nswa, 1), 32])
    cs_d = din("cs_tab", [128, L])
    oh_d = din("onehot", [33, OH_N])
    yT = nc.dram_tensor("yT", [NSEQ, D, S], F32, kind="ExternalOutput").ap()

    Wb = dscr("W2048b", [cfg.NM, 128, KC * 128], BF16)
    Woutb = dscr("Woutb", [cfg.depth * 16, 128, 64 * 128], BF16)
    Wuqb = dscr("Wuqb", [cfg.nmla * 16, 128, 1024], BF16)
    Wukvb = dscr("Wukvb", [cfg.nmla * 16, 128, 1024], BF16)
    if cfg.dbg:
        hscr = nc.dram_tensor("hscr", [NSEQ, D, L], F32, kind="ExternalOutput").ap()
    else:
        hscr = dscr("hscr", [NSEQ, D, L], F32)
    oscr = dscr("oscr", [NSEQ, D, L], BF16)
    Gs = dscr("Gs", [32, OH_N], BF16)
    EBm_d = dscr("EBm_d", [128, 3, 32, 128], BF16)
    EBk0_d = dscr("EBk0_d", [16, 32, 128], BF16)
    EBkc_d = dscr("EBkc_d", [16, 32, 128], BF16)
    EBq4_d = dscr("EBq4_d", [16, 32, 16], BF16)
    EBq5_d = dscr("EBq5_d", [128, 32, 16], BF16)

    NGAIN = cfg.depth * 64 + cfg.nmla * 8

    with contextlib.ExitStack() as top:
        ar = Arena(nc, 18432, nc.sbuf_top - 64)
        PS = [top.enter_context(nc.psum_tensor("ps%d" % i, [128, 512], F32)) for i in range(8)]
        PSB = [Buf("ps%d" % i) for i in range(8)]

        def bankring(ids):
            return Ring([(PS[i], PSB[i]) for i in ids])

        ones_bf = ar.alloc([128, 128], BF16)
        ones_f = ar.alloc([128, 128], F32)
        gains = ar.alloc([128, NGAIN], F32)
        b_const = Buf("const")
        P.op("pool", lambda e: e.memset(ones_bf[:], 1.0), writes=[b_const])
        P.op("pool", lambda e: e.memset(ones_f[:], 1.0), writes=[b_const])
        P.op("sp", lambda e: e.dma_start(out=gains[:], in_=gains_d[:, :]), writes=[b_const], dma="const")
        P.barrier()
        b_const.const = True
        stage_base = ar.mark()

        hbufs = [[[Buf("h%d_%d_%d" % (q, s, c)) for c in range(4)] for s in range(NST)] for q in range(NSEQ)]
        obufs = [[Buf("o%d_%d" % (q, s)) for s in range(NST)] for q in range(NSEQ)]
        b_wscr = Buf("wscr")

        def hview(ap2d):
            return ap2d.rearrange("(c p) l -> p c l", p=128)

        def pp_unit_list():
            early, late = [], []

            def w2048(i0, cnt, lst):
                i = i0
                while i < i0 + cnt:
                    k = min(2, i0 + cnt - i)
                    lst.append((Wf[i:i + k].rearrange("m p e -> p m e"), Wb[i:i + k].rearrange("m p e -> p m e"),
                                k, 2048))
                    i += k

            def small(srcT, dstT, i0, cnt, lst):
                i = i0
                while i < i0 + cnt:
                    k = min(4, i0 + cnt - i)
                    lst.append((srcT[i:i + k].rearrange("m p e -> p m e"), dstT[i:i + k].rearrange("m p e -> p m e"),
                                k, 1024))
                    i += k

            def wout(layer, lst):
                for dmc in range(16):
                    i = layer * 16 + dmc
                    for hf in range(2):
                        lst.append((Woutf[i:i + 1, :, hf * 4096:(hf + 1) * 4096].rearrange("m p e -> p m e"),
                                    Woutb[i:i + 1, :, hf * 4096:(hf + 1) * 4096].rearrange("m p e -> p m e"), 1, 4096))

            for layer in range(cfg.depth):
                jj = layer // 2
                if layer % 2 == 0:
                    lst = early if layer == 0 else late
                    m0, c = cfg.mch[(layer, "dq")]
                    w2048(m0, c, lst)
                    m0, c = cfg.mch[(layer, "dkv")]
                    w2048(m0, c, lst)
                    small(Wuqf, Wuqb, jj * 16, 16, lst)
                    small(Wukvf, Wukvb, jj * 16, 16, lst)
                else:
                    m0, c = cfg.mch[(layer, "kv")]
                    w2048(m0, c, late)
                    m0, c = cfg.mch[(layer, "q")]
                    w2048(m0, c, late)
                m0, c = cfg.mch[(layer, "wo")]
                w2048(m0, c, late)
                m0, c = cfg.mch[(layer, "win")]
                w2048(m0, c, late)
                wout(layer, late)
            return early, late

        pp_late = []

        def pp_alloc():
            fin = Ring([(ar.alloc([128, 4096], F32), Buf("pi%d" % i)) for i in range(2)])
            fout = Ring([(ar.alloc([128, 4096], BF16), Buf("po%d" % i)) for i in range(2)])
            return fin, fout

        def pp_emit(unit, fin, fout, eng):
            src, dst, a, b = unit
            fi, bi = fin.next()
            fo, bo = fout.next()
            ne = a * b
            P.op("sp", lambda e: e.dma_start(out=fi[:, 0:ne].rearrange("p (m e) -> p m e", m=a), in_=src),
                 writes=[bi], dma="L" + bi.name)
            if eng == "act":
                P.op("act", lambda e: e.copy(out=fo[:, 0:ne], in_=fi[:, 0:ne]), reads=[bi], writes=[bo])
            else:
                P.op(eng, lambda e: e.tensor_copy(out=fo[:, 0:ne], in_=fi[:, 0:ne]), reads=[bi], writes=[bo])
            P.op("pool", lambda e: e.dma_start(out=dst, in_=fo[:, 0:ne].rearrange("p (m e) -> p m e", m=a)),
                 reads=[bo], dma="S" + bo.name)

        def prepass():
            m = ar.mark()
            early, late = pp_unit_list()
            pp_late.extend(late)
            fin, fout = pp_alloc()
            engs = ["dve", "act", "pool"]
            for i, u in enumerate(early):
                pp_emit(u, fin, fout, engs[i % 3])
            P.barrier()
            b_wscr.const = True
            ar.release(m)

        class Ctx:
            pass

        def alloc_ring(nslots):
            items = []
            for _ in range(nslots):
                a = ar.alloc([128, 4, KC, 128], BF16)
                off = ar.off - 16384
                ar.n += 1
                b = nc.alloc_sbuf_tensor_at("t%d" % ar.n, [128, 64, 128], BF16, offset=off)
                items.append((a, b, Buf("w%d" % len(items))))
            return Ring(items)

        dmak = [0]

        def dkey(prefix, n=4):
            dmak[0] += 1
            return "%s%d" % (prefix, dmak[0] % n)

        def src_h(seq, use_x):
            if use_x:
                xv = hview(xT[seq])
                mv = hview(metaT)

                def f(col0, n, c0, ncn):
                    if col0 >= S:
                        return mv[:, c0:c0 + ncn, col0 - S:col0 - S + n]
                    return xv[:, c0:c0 + ncn, col0:col0 + n]
                return f
            hv = hview(hscr[seq])
            return lambda col0, n, c0, ncn: hv[:, c0:c0 + ncn, col0:col0 + n]

        def dst_h(seq, last):
            if last:
                yv = hview(yT[seq])

                def f(col0, n, c0, ncn):
                    if col0 >= S:
                        return None
                    return yv[:, c0:c0 + ncn, col0:col0 + n]
                return f
            hv = hview(hscr[seq])
            return lambda col0, n, c0, ncn: hv[:, c0:c0 + ncn, col0:col0 + n]

        def norm_stats(cx, srcs, n, nchunks, inv_dim):
            bank, bb = cx.nbank.next()
            for c in range(nchunks):
                apf, sb = srcs[c]
                sq, sqb = cx.sq.next()
                P.op("act", lambda e, apf=apf, sq=sq: e.activation(out=sq[:, 0, 0:n], in_=apf(), func=AF.Square),
                     reads=[sb], writes=[sqb])
                P.op("pe", lambda e, sq=sq, bank=bank, c=c: e.matmul(
                    bank[:, 0:n], lhsT=ones_bf[:, :], rhs=sq[:, 0, 0:n], start=(c == 0), stop=(c == nchunks - 1)),
                    reads=[sqb, b_const], writes=[bb])
            rs, rsb = cx.rstd.next()
            P.op("act", lambda e: e.activation(out=rs[:, 0:n], in_=bank[:, 0:n], func=AF.Ln, scale=inv_dim, bias=EPS),
                 reads=[bb], writes=[rsb])
            P.op("act", lambda e: e.activation(out=rs[:, 0:n], in_=rs[:, 0:n], func=AF.Exp, scale=-0.5),
                 reads=[rsb], writes=[rsb])
            return rs, rsb

        def load_norm(cx, seq, use_x, s, gcol, xn, xnb):
            sf = src_h(seq, use_x)
            hbq = hbufs[seq][s]
            for si, (col0, n) in enumerate(cfg.segs(s)):
                xo = 0 if si == 0 else 512
                bank, bb = cx.nbank.next()
                for q in range(4):
                    hq, hqb = cx.hq.next()
                    P.op("sp", lambda e, hq=hq, q=q, col0=col0, n=n: e.dma_start(
                        out=hq[:, :, 0:n], in_=sf(col0, n, 4 * q, 4)), reads=[hbq[q]], writes=[hqb], dma="L" + hqb.name)
                    sq, sqb = cx.sq.next()
                    P.op("act", lambda e, hq=hq, sq=sq, n=n: e.activation(out=sq[:, :, 0:n], in_=hq[:, :, 0:n],
                                                                         func=AF.Square), reads=[hqb], writes=[sqb])
                    for c in range(4):
                        P.op("pe", lambda e, sq=sq, bank=bank, c=c, q=q, n=n: e.matmul(
                            bank[:, 0:n], lhsT=ones_bf[:, :], rhs=sq[:, c, 0:n], start=(q == 0 and c == 0),
                            stop=(q == 3 and c == 3)), reads=[sqb, b_const], writes=[bb])
                rs, rsb = cx.rstd.next()
                P.op("act", lambda e, rs=rs, bank=bank, n=n: e.activation(
                    out=rs[:, 0:n], in_=bank[:, 0:n], func=AF.Ln, scale=1.0 / D, bias=EPS), reads=[bb], writes=[rsb])
                P.op("act", lambda e, rs=rs, n=n: e.activation(out=rs[:, 0:n], in_=rs[:, 0:n], func=AF.Exp,
                                                               scale=-0.5), reads=[rsb], writes=[rsb])
                for q in range(4):
                    hq, hqb = cx.hq.next()
                    P.op("sp", lambda e, hq=hq, q=q, col0=col0, n=n: e.dma_start(
                        out=hq[:, :, 0:n], in_=sf(col0, n, 4 * q, 4)), reads=[hbq[q]], writes=[hqb], dma="L" + hqb.name)
                    for c in range(4):
                        cc = 4 * q + c
                        P.op("dve", lambda e, hq=hq, c=c, cc=cc, rs=rs, n=n, xo=xo: e.scalar_tensor_tensor(
                            out=xn[:, cc, xo:xo + n], in0=hq[:, c, 0:n], scalar=gains[:, gcol + cc:gcol + cc + 1],
                            in1=rs[:, 0:n], op0=ALU.mult, op1=ALU.mult), reads=[hqb, rsb, b_const], writes=[xnb[si]])

        def load_norm_a(cx, seq, use_x, s):
            sf = src_h(seq, use_x)
            hbq = hbufs[seq][s]
            st = []
            for si, (col0, n) in enumerate(cfg.segs(s)):
                sqs = []
                for q in range(4):
                    hq, hqb = cx.hq.next()
                    P.op("sp", lambda e, hq=hq, q=q, col0=col0, n=n: e.dma_start(
                        out=hq[:, :, 0:n], in_=sf(col0, n, 4 * q, 4)), reads=[hbq[q]], writes=[hqb], dma="L" + hqb.name)
                    sq, sqb = cx.sq.next()
                    P.op("act", lambda e, hq=hq, sq=sq, n=n: e.activation(out=sq[:, :, 0:n], in_=hq[:, :, 0:n],
                                                                         func=AF.Square), reads=[hqb], writes=[sqb])
                    sqs.append((sq, sqb))
                st.append(sqs)
            return st

        def load_norm_b(cx, seq, use_x, s, gcol, xn, xnb, st):
            sf = src_h(seq, use_x)
            hbq = hbufs[seq][s]
            for si, (col0, n) in enumerate(cfg.segs(s)):
                xo = 0 if si == 0 else 512
                bank, bb = cx.nbank.next()
                for q in range(4):
                    sq, sqb = st[si][q]
                    for c in range(4):
                        P.op("pe", lambda e, sq=sq, bank=bank, c=c, q=q, n=n: e.matmul(
                            bank[:, 0:n], lhsT=ones_bf[:, :], rhs=sq[:, c, 0:n], start=(q == 0 and c == 0),
                            stop=(q == 3 and c == 3)), reads=[sqb, b_const], writes=[bb])
                rs, rsb = cx.rstd.next()
                P.op("act", lambda e, rs=rs, bank=bank, n=n: e.activation(
                    out=rs[:, 0:n], in_=bank[:, 0:n], func=AF.Ln, scale=1.0 / D, bias=EPS), reads=[bb], writes=[rsb])
                P.op("act", lambda e, rs=rs, n=n: e.activation(out=rs[:, 0:n], in_=rs[:, 0:n], func=AF.Exp,
                                                               scale=-0.5), reads=[rsb], writes=[rsb])
                for q in range(4):
                    hq, hqb = cx.hq.next()
                    P.op("pool", lambda e, hq=hq, q=q, col0=col0, n=n: e.dma_start(
                        out=hq[:, :, 0:n], in_=sf(col0, n, 4 * q, 4)), reads=[hbq[q]], writes=[hqb], dma="L" + hqb.name)
                    for c in range(4):
                        cc = 4 * q + c
                        P.op("dve", lambda e, hq=hq, c=c, cc=cc, rs=rs, n=n, xo=xo: e.scalar_tensor_tensor(
                            out=xn[:, cc, xo:xo + n], in0=hq[:, c, 0:n], scalar=gains[:, gcol + cc:gcol + cc + 1],
                            in1=rs[:, 0:n], op0=ALU.mult, op1=ALU.mult), reads=[hqb, rsb, b_const], writes=[xnb[si]])

        def post_norm_resid(cx, seq, use_x, s, gcol, fT, fTb, last):
            sf = src_h(seq, use_x)
            df = dst_h(seq, last)
            hbq = hbufs[seq][s]
            for si, (col0, n) in enumerate(cfg.segs(s)):
                xo = 0 if si == 0 else 512
                bank, bb = cx.nbank.next()
                for q in range(4):
                    sq, sqb = cx.sq.next()
                    P.op("act", lambda e, sq=sq, q=q, n=n, xo=xo: e.activation(
                        out=sq[:, :, 0:n], in_=fT[:, 4 * q:4 * q + 4, xo:xo + n], func=AF.Square),
                        reads=[fTb[si]], writes=[sqb])
                    for c in range(4):
                        P.op("pe", lambda e, sq=sq, bank=bank, c=c, q=q, n=n: e.matmul(
                            bank[:, 0:n], lhsT=ones_bf[:, :], rhs=sq[:, c, 0:n], start=(q == 0 and c == 0),
                            stop=(q == 3 and c == 3)), reads=[sqb, b_const], writes=[bb])
                rs, rsb = cx.rstd.next()
                P.op("act", lambda e, rs=rs, bank=bank, n=n: e.activation(
                    out=rs[:, 0:n], in_=bank[:, 0:n], func=AF.Ln, scale=1.0 / D, bias=EPS), reads=[bb], writes=[rsb])
                P.op("act", lambda e, rs=rs, n=n: e.activation(out=rs[:, 0:n], in_=rs[:, 0:n], func=AF.Exp,
                                                               scale=-0.5), reads=[rsb], writes=[rsb])
                for q in range(4):
                    dst = df(col0, n, 4 * q, 4)
                    if dst is None:
                        continue
                    hq, hqb = cx.hq.next()
                    P.op("pool", lambda e, hq=hq, q=q, col0=col0, n=n: e.dma_start(
                        out=hq[:, :, 0:n], in_=sf(col0, n, 4 * q, 4)), reads=[hbq[q]], writes=[hqb], dma="L" + hqb.name)
                    for c in range(4):
                        cc = 4 * q + c
                        P.op("dve", lambda e, cc=cc, rs=rs, n=n, xo=xo: e.scalar_tensor_tensor(
                            out=fT[:, cc, xo:xo + n], in0=fT[:, cc, xo:xo + n],
                            scalar=gains[:, gcol + cc:gcol + cc + 1], in1=rs[:, 0:n], op0=ALU.mult, op1=ALU.mult),
                            reads=[fTb[si], rsb, b_const], writes=[fTb[si]])
                    P.op("pool", lambda e, hq=hq, q=q, n=n, xo=xo: e.tensor_tensor(
                        out=hq[:, :, 0:n], in0=hq[:, :, 0:n], in1=fT[:, 4 * q:4 * q + 4, xo:xo + n], op=ALU.add),
                        reads=[hqb, fTb[si]], writes=[hqb])
                    P.op("pool", lambda e, hq=hq, dst=dst, n=n: e.dma_start(out=dst, in_=hq[:, :, 0:n]),
                         reads=[hqb], writes=[hbq[q]], dma="S" + hqb.name)

        def proj2048(cx, m0, cnt, xin_fn, xin_bufs, segs, epi):
            g0 = 0
            while g0 < cnt:
                k = min(4, cnt - g0)
                wa, _, wb = cx.ring.next()
                P.op("sp", lambda e, wa=wa, g0=g0, k=k: e.dma_start(
                    out=wa[:, 0:k, :, :].rearrange("p m k c -> p m (k c)"),
                    in_=Wb[m0 + g0:m0 + g0 + k].rearrange("m p e -> p m e")),
                    reads=[b_wscr], writes=[wb], dma="L" + wb.name)
                for mi in range(k):
                    for si, (col0, n) in enumerate(segs):
                        bank, bb = cx.mbank.next()
                        for kc in range(KC):
                            P.op("pe", lambda e, wa=wa, mi=mi, kc=kc, bank=bank, si=si, n=n: e.matmul(
                                bank[:, 0:n], lhsT=wa[:, mi, kc, :], rhs=xin_fn(si, kc, n), start=(kc == 0),
                                stop=(kc == KC - 1)), reads=[wb, xin_bufs[si]], writes=[bb])
                        epi(g0 + mi, si, n, bank, bb)
                g0 += k

        evk = [0]

        def evac(out_fn, bank, bb, n, wbufs, rows=128):
            evk[0] += 1
            if evk[0] % 2 == 0:
                P.op("dve", lambda e: e.tensor_copy(out=out_fn(), in_=bank[0:rows, 0:n]), reads=[bb], writes=wbufs)
            else:
                P.op("act", lambda e: e.copy(out=out_fn(), in_=bank[0:rows, 0:n]), reads=[bb], writes=wbufs)

        def common_ctx(nring, nhq=2, nsq=2, nrstd=2):
            cx = Ctx()
            cx.ring = alloc_ring(nring) if nring else None
            cx.hq = Ring([(ar.alloc([128, 4, 528], F32), Buf("hq%d" % i)) for i in range(nhq)])
            cx.sq = Ring([(ar.alloc([128, 4, 528], BF16), Buf()) for _ in range(nsq)])
            cx.rstd = Ring([(ar.alloc([128, 512], F32), Buf()) for _ in range(nrstd)])
            cx.nbank = bankring([6, 7])
            cx.mbank = bankring([0, 1, 2, 3, 4, 5])
            return cx

        def stage_C(seq, layer):
            m = ar.mark()
            cx = common_ctx(3)
            fTs = [(ar.alloc([128, KC, 528], F32), [Buf(), Buf()]) for _ in range(2)]
            xin = [(ar.alloc([128, KC, 528], BF16), [Buf("xi%d_0" % i), Buf("xi%d_1" % i)]) for i in range(2)]
            m0, cnt = cfg.mch[(layer, "wo")]
            ov = hview(oscr[seq])
            for s in range(NST):
                segs = cfg.segs(s)
                xi, xib = xin[s % 2]
                fT, fTb = fTs[s % 2]
                for si, (col0, n) in enumerate(segs):
                    xo = 0 if si == 0 else 512
                    P.op("sp", lambda e, xi=xi, col0=col0, n=n, xo=xo: e.dma_start(
                        out=xi[:, :, xo:xo + n], in_=ov[:, :, col0:col0 + n]), reads=[obufs[seq][s]],
                        writes=[xib[si]], dma="L" + xib[si].name)

                def xin_fn(si, kc, n, xi=xi):
                    xo = 0 if si == 0 else 512
                    return xi[:, kc, xo:xo + n]

                def epi(mi, si, n, bank, bb, fTb=fTb, fT=fT):
                    xo = 0 if si == 0 else 512
                    evac(lambda: fT[:, mi, xo:xo + n], bank, bb, n, [fTb[si]])

                proj2048(cx, m0, cnt, xin_fn, xib, segs, epi)
                post_norm_resid(cx, seq, layer == 0, s, gain_col(cfg, layer, "mix_post"), fT, fTb, False)
            P.barrier()
            ar.release(m)

        def stage_D(seq, layer):
            m = ar.mark()
            cx = common_ctx(3, nsq=4, nrstd=1)
            fT = ar.alloc([128, KC, 528], F32)
            xn = ar.alloc([128, KC, 528], BF16)
            hff = ar.alloc([128, 64, 528], BF16)
            rr = Ring([(ar.alloc([128, 512], F32), Buf()) for _ in range(2)])
            m0, cnt = cfg.mch[(layer, "win")]
            last = (layer == cfg.depth - 1)
            xnb = [Buf(), Buf()]
            hfb = [Buf(), Buf()]
            fTb = [Buf(), Buf()]
            load_norm(cx, seq, False, 0, gain_col(cfg, layer, "mlp_pre"), xn, xnb)
            for s in range(NST):
                segs = cfg.segs(s)

                def xin_fn(si, kc, n):
                    xo = 0 if si == 0 else 512
                    return xn[:, kc, xo:xo + n]

                def epi(mi, si, n, bank, bb, hfb=hfb):
                    xo = 0 if si == 0 else 512
                    r, rb = rr.next()
                    P.op("act", lambda e: e.activation(out=r[:, 0:n], in_=bank[:, 0:n], func=AF.Relu),
                         reads=[bb], writes=[rb])
                    P.op("pool", lambda e: e.tensor_tensor(out=hff[:, mi, xo:xo + n], in0=r[:, 0:n], in1=r[:, 0:n],
                                                          op=ALU.mult), reads=[rb], writes=[hfb[si]])

                proj2048(cx, m0, cnt, xin_fn, xnb, segs, epi)
                for dmc in range(16):
                    _, wbt, wb = cx.ring.next()
                    P.op("sp", lambda e, wbt=wbt, dmc=dmc: e.dma_start(
                        out=wbt[:, :, :].rearrange("p f c -> p (f c)"), in_=Woutb[layer * 16 + dmc]),
                        reads=[b_wscr], writes=[wb], dma="L" + wb.name)
                    if s + 1 < NST and len(cfg.segs(s + 1)) == 1:
                        if dmc == 2:
                            ln_st = load_norm_a(cx, seq, False, s + 1)
                        if dmc == 7:
                            load_norm_b(cx, seq, False, s + 1, gain_col(cfg, layer, "mlp_pre"), xn, xnb, ln_st)
                    elif s + 1 < NST and dmc == 3:
                        load_norm(cx, seq, False, s + 1, gain_col(cfg, layer, "mlp_pre"), xn, xnb)
                    for si, (col0, n) in enumerate(segs):
                        xo = 0 if si == 0 else 512
                        bank, bb = cx.mbank.next()
                        for fc in range(64):
                            P.op("pe", lambda e, wbt=wbt, fc=fc, bank=bank, n=n, xo=xo: e.matmul(
                                bank[:, 0:n], lhsT=wbt[:, fc, :], rhs=hff[:, fc, xo:xo + n], start=(fc == 0),
                                stop=(fc == 63)), reads=[wb, hfb[si]], writes=[bb])
                        evac(lambda dmc=dmc, xo=xo, n=n: fT[:, dmc, xo:xo + n], bank, bb, n, [fTb[si]])
                post_norm_resid(cx, seq, False, s, gain_col(cfg, layer, "mlp_post"), fT, fTb, last)
            P.barrier()
            ar.release(m)

        def stage_MLA(seq, layer):
            j = layer // 2
            m = ar.mark()
            cqT = ar.alloc([128, 4, L], BF16)
            ckvT = ar.alloc([128, 4, L], BF16)
            krT = ar.alloc([128, L], BF16)
            cs = ar.alloc([128, L], F32)
            b_cq = [Buf() for _ in range(NST)]
            b_ckv = [Buf() for _ in range(NST)]
            b_kr = [Buf() for _ in range(NST)]
            b_cs = Buf()
            P.op("sp", lambda e: e.dma_start(out=cs[:], in_=cs_d[:, :]), writes=[b_cs], dma="cs")
            P.op("pool", lambda e: e.memset(krT[64:128, :], 0.0), writes=b_kr)
            mA = ar.mark()
            cx = common_ctx(3)
            xn = ar.alloc([128, KC, 528], BF16)
            pre = Ring([(ar.alloc([128, 4, 528], F32), [Buf(), Buf()]) for _ in range(2)])
            prod = ar.alloc([128, 528], F32)
            tmp = ar.alloc([64, 528], F32)
            b_prod = Buf()
            b_tmp = Buf()
            mdq, _ = cfg.mch[(layer, "dq")]
            mdkv, _ = cfg.mch[(layer, "dkv")]
            xnb = [Buf(), Buf()]
            for s in range(NST):
                segs = cfg.segs(s)
                load_norm(cx, seq, layer == 0, s, gain_col(cfg, layer, "mix_pre"), xn, xnb)

                def xin_fn(si, kc, n):
                    xo = 0 if si == 0 else 512
                    return xn[:, kc, xo:xo + n]

                for which in ("q", "kv"):
                    pr, prb = pre.next()

                    def epi(mi, si, n, bank, bb, pr=pr, prb=prb, s=s):
                        xo = 0 if si == 0 else 512
                        col0 = segs[si][0]
                        if mi < 4:
                            evac(lambda: pr[:, mi, xo:xo + n], bank, bb, n, [prb[si]])
                        else:
                            P.op("dve", lambda e: e.tensor_tensor(out=prod[:, 0:n], in0=bank[:, 0:n],
                                                                  in1=cs[:, col0:col0 + n], op=ALU.mult),
                                 reads=[bb, b_cs], writes=[b_prod])
                            P.op("dve", lambda e: e.tensor_copy(out=tmp[0:64, 0:n], in_=prod[64:128, 0:n]),
                                 reads=[b_prod], writes=[b_tmp])
                            P.op("dve", lambda e: e.tensor_tensor(out=krT[0:64, col0:col0 + n], in0=prod[0:64, 0:n],
                                                                  in1=tmp[0:64, 0:n], op=ALU.add),
                                 reads=[b_prod, b_tmp], writes=[b_kr[s]])

                    if which == "q":
                        proj2048(cx, mdq, 4, xin_fn, xnb, segs, epi)
                        dstT, dstb, gc = cqT, b_cq, lat_gain_col(cfg, j, "q")
                    else:
                        proj2048(cx, mdkv, 5, xin_fn, xnb, segs, epi)
                        dstT, dstb, gc = ckvT, b_ckv, lat_gain_col(cfg, j, "kv")
                    for si, (col0, n) in enumerate(segs):
                        xo = 0 if si == 0 else 512
                        srcs = [((lambda c=c, xo=xo, n=n, pr=pr: pr[:, c, xo:xo + n]), prb[si]) for c in range(4)]
                        rs, rsb = norm_stats(cx, srcs, n, 4, 1.0 / 512)
                        for c in range(4):
                            P.op("dve", lambda e, c=c, xo=xo, n=n, col0=col0, pr=pr, rs=rs, dstT=dstT, gc=gc:
                                 e.scalar_tensor_tensor(out=dstT[:, c, col0:col0 + n], in0=pr[:, c, xo:xo + n],
                                                        scalar=gains[:, gc + c:gc + c + 1], in1=rs[:, 0:n],
                                                        op0=ALU.mult, op1=ALU.mult),
                                 reads=[prb[si], rsb, b_const], writes=[dstb[s]])
            P.barrier()
            ar.release(mA)
            KT = cfg.ktiles()
            NKT = len(KT)
            knT = Ring([(ar.alloc([128, L], BF16), Buf()) for _ in range(2)])
            vh = Ring([(ar.alloc([128, NKT, 128], BF16), Buf()) for _ in range(2)])
            wq = Ring([(ar.alloc([128, 4, 256], BF16), Buf("wq%d" % i)) for i in range(2)])
            wkv = Ring([(ar.alloc([128, 4, 256], BF16), Buf("wk%d" % i)) for i in range(2)])
            qn = Ring([(ar.alloc([128, 512], BF16), Buf()) for _ in range(2)])
            qr = Ring([(ar.alloc([128, 512], BF16), Buf()) for _ in range(2)])
            for (qrt_, qrb_) in qr.items:
                P.op("pool", lambda e, qrt_=qrt_: e.memset(qrt_[64:128, :], 0.0), writes=[qrb_])
            acc = Ring([(ar.alloc([128, 512], F32), Buf()) for _ in range(2)])
            prod = ar.alloc([128, 512], F32)
            tmp = ar.alloc([64, 512], F32)
            rden = ar.alloc([128, 512], F32)
            ob = Ring([(ar.alloc([128, 512], BF16), Buf("ob%d" % i)) for i in range(2)])
            b_prod, b_tmp, b_rden = Buf(), Buf(), Buf()
            sbank = bankring([0, 1, 2])
            obank = bankring([3, 4])
            dbank = bankring([5, 6])
            xbank = bankring([7])
            scale = (128 + 64) ** -0.5
            ovh = oscr[seq]
            allcq = b_cq
            allckv = b_ckv
            acck = [0]
            LA = 3
            pT = Ring([(ar.alloc([128, 512], BF16), Buf()) for _ in range(5)])

            def head_setup(h):
                wqt, wqb = wq.next()
                wkt, wkb = wkv.next()
                P.op("sp", lambda e: e.dma_start(
                    out=wqt[:, :, :].rearrange("p k c -> p (k c)"), in_=Wuqb[j * 16 + h]), reads=[b_wscr],
                    writes=[wqb], dma="L" + wqb.name)
                P.op("sp", lambda e: e.dma_start(
                    out=wkt[:, :, :].rearrange("p k c -> p (k c)"), in_=Wukvb[j * 16 + h]), reads=[b_wscr],
                    writes=[wkb], dma="L" + wkb.name)
                kn, knb = knT.next()
                vt, vtb = vh.next()
                for s in range(NST):
                    for (col0, n) in cfg.segs(s):
                        bank, bb = xbank.next()
                        for kc in range(4):
                            P.op("pe", lambda e, kc=kc, bank=bank, col0=col0, n=n: e.matmul(
                                bank[:, 0:n], lhsT=wkt[:, kc, 0:128], rhs=ckvT[:, kc, col0:col0 + n],
                                start=(kc == 0), stop=(kc == 3)), reads=[wkb, allckv[s]], writes=[bb])
                        P.op("dve", lambda e, bank=bank, col0=col0, n=n: e.tensor_copy(
                            out=kn[:, col0:col0 + n], in_=bank[:, 0:n]), reads=[bb], writes=[knb])
                for t0 in range(0, NKT, 4):
                    bank, bb = xbank.next()
                    tl = KT[t0:t0 + 4]
                    for ti, (col0, n) in enumerate(tl):
                        s = min(col0 // 512, NST - 1)
                        for kc in range(4):
                            P.op("pe", lambda e, kc=kc, bank=bank, col0=col0, n=n, ti=ti: e.matmul(
                                bank[0:n, ti * 128:(ti + 1) * 128], lhsT=ckvT[:, kc, col0:col0 + n],
                                rhs=wkt[:, kc, 128:256], start=(kc == 0), stop=(kc == 3)),
                                reads=[wkb, allckv[s]], writes=[bb])
                    for ti, (col0, n) in enumerate(tl):
                        P.op("dve", lambda e, bank=bank, t=t0 + ti, ti=ti, n=n: e.tensor_copy(
                            out=vt[0:n, t, :], in_=bank[0:n, ti * 128:(ti + 1) * 128]), reads=[bb], writes=[vtb])
                return dict(wqt=wqt, wqb=wqb, kn=kn, knb=knb, vt=vt, vtb=vtb)

            def build_q(item, H):
                h, s, q0, nq = item
                wqt, wqb = H["wqt"], H["wqb"]
                qnt, qnb = qn.next()
                qrt, qrb = qr.next()
                bank, bb = xbank.next()
                for kc in range(4):
                    P.op("pe", lambda e, kc=kc: e.matmul(
                        bank[:, 0:nq], lhsT=wqt[:, kc, 0:128], rhs=cqT[:, kc, q0:q0 + nq],
                        start=(kc == 0), stop=(kc == 3)), reads=[wqb, allcq[s]], writes=[bb])
                P.op("dve", lambda e: e.tensor_copy(out=qnt[:, 0:nq], in_=bank[:, 0:nq]), reads=[bb], writes=[qnb])
                bank2, bb2 = xbank.next()
                for kc in range(4):
                    P.op("pe", lambda e, kc=kc: e.matmul(
                        bank2[:, 0:nq], lhsT=wqt[:, kc, 128:256], rhs=cqT[:, kc, q0:q0 + nq],
                        start=(kc == 0), stop=(kc == 3)), reads=[wqb, allcq[s]], writes=[bb2])
                P.op("dve", lambda e: e.tensor_tensor(
                    out=prod[:, 0:nq], in0=bank2[:, 0:nq], in1=cs[:, q0:q0 + nq], op=ALU.mult),
                    reads=[bb2, b_cs], writes=[b_prod])
                P.op("dve", lambda e: e.tensor_copy(out=tmp[0:64, 0:nq], in_=prod[64:128, 0:nq]),
                     reads=[b_prod], writes=[b_tmp])
                P.op("dve", lambda e: e.tensor_tensor(
                    out=qrt[0:64, 0:nq], in0=prod[0:64, 0:nq], in1=tmp[0:64, 0:nq], op=ALU.add),
                    reads=[b_prod, b_tmp], writes=[qrb])
                return (qnt, qnb, qrt, qrb)

            def make_finish(item, po, pob, ac, acb, pd, pdb):
                h, s, q0, nq = item

                def fin():
                    bank3, bb3 = pd, pdb
                    P.op("pe", lambda e: e.matmul(
                        bank3[:, 0:nq], lhsT=ones_f[:, :], rhs=ac[:, 0:nq], start=False, stop=True),
                        reads=[acb, b_const], writes=[bb3])
                    P.op("dve", lambda e: e.reciprocal(out=rden[:, 0:nq], in_=bank3[:, 0:nq]),
                         reads=[bb3], writes=[b_rden])
                    obt, obb = ob.next()
                    P.op("dve", lambda e: e.tensor_tensor(
                        out=obt[:, 0:nq], in0=po[:, 0:nq], in1=rden[:, 0:nq], op=ALU.mult),
                        reads=[pob, b_rden], writes=[obb])
                    P.op("pool", lambda e: e.dma_start(
                        out=ovh[h * 128:(h + 1) * 128, q0:q0 + nq], in_=obt[:, 0:nq]), reads=[obb],
                        writes=[obufs[seq][s]], dma="S" + obb.name)
                return fin

            items = []
            for h in range(16):
                for s in range(NST):
                    for (q0, nq) in cfg.segs(s):
                        items.append((h, s, q0, nq))
            hd = {0: head_setup(0)}
            qctx = {0: build_q(items[0], hd[0])}
            prev_fin = None
            if pp_late:
                ppf, ppo = pp_alloc()
                pp_per_item = -(-len(pp_late) // max(1, len(items) - 8))
            for i, item in enumerate(items):
                if pp_late and i >= 2:
                    for _ in range(pp_per_item):
                        if pp_late:
                            pp_emit(pp_late.pop(0), ppf, ppo, "pool")
                h, s, q0, nq = item
                H = hd[h]
                kn, knb, vt, vtb = H["kn"], H["knb"], H["vt"], H["vtb"]
                qnt, qnb, qrt, qrb = qctx.pop(i)
                po, pob = obank.next()
                pd, pdb = dbank.next()
                ac, acb = acc.next()
                sts = {}

                def emitST(t, kn=kn, knb=knb, qnt=qnt, qnb=qnb, qrt=qrt, qrb=qrb, nq=nq, sts=sts):
                    k0, nk = KT[t]
                    sk = min(k0 // 512, NST - 1)
                    sb_, sbb = sbank.next()
                    P.op("pe", lambda e: e.matmul(
                        sb_[0:nk, 0:nq], lhsT=kn[:, k0:k0 + nk], rhs=qnt[:, 0:nq], start=True, stop=False),
                        reads=[knb, qnb], writes=[sbb])
                    P.op("pe", lambda e: e.matmul(
                        sb_[0:nk, 0:nq], lhsT=krT[:, k0:k0 + nk], rhs=qrt[:, 0:nq], start=False,
                        stop=True), reads=[b_kr[sk], qrb], writes=[sbb])
                    sts[t] = (sb_, sbb)

                for t in range(min(LA, NKT)):
                    emitST(t)
                if i + 1 < len(items):
                    h2 = items[i + 1][0]
                    if h2 != h:
                        hd[h2] = head_setup(h2)
                    qctx[i + 1] = build_q(items[i + 1], hd[h2])
                if prev_fin is not None:
                    prev_fin()
                    prev_fin = None
                for t, (k0, nk) in enumerate(KT):
                    sb_, sbb = sts.pop(t)
                    pt, ptb = pT.next()
                    P.op("act", lambda e, pt=pt, sb_=sb_, nk=nk, nq=nq: e.activation(
                        out=pt[0:nk, 0:nq], in_=sb_[0:nk, 0:nq], func=AF.Exp, scale=scale),
                        reads=[sbb], writes=[ptb])
                    if t + LA < NKT:
                        emitST(t + LA)
                    P.op("pe", lambda e, po=po, vt=vt, pt=pt, t=t, nk=nk, nq=nq: e.matmul(
                        po[:, 0:nq], lhsT=vt[0:nk, t, :], rhs=pt[0:nk, 0:nq], start=(t == 0),
                        stop=(t == NKT - 1)), reads=[vtb, ptb], writes=[pob])
                    if t % 2 == 0:
                        if t == 0:
                            P.op("dve", lambda e, ac=ac, pt=pt, nq=nq: e.tensor_copy(out=ac[:, 0:nq], in_=pt[:, 0:nq]),
                                 reads=[ptb], writes=[acb])
                        else:
                            P.op("dve", lambda e, ac=ac, pt=pt, nk=nk, nq=nq: e.tensor_tensor(
                                out=ac[0:nk, 0:nq], in0=ac[0:nk, 0:nq], in1=pt[0:nk, 0:nq], op=ALU.add),
                                reads=[ptb, acb], writes=[acb])
                    else:
                        P.op("pe", lambda e, pd=pd, pt=pt, t=t, nk=nk, nq=nq: e.matmul(
                            pd[:, 0:nq], lhsT=ones_bf[0:nk, :], rhs=pt[0:nk, 0:nq], start=(t == 1), stop=False),
                            reads=[ptb, b_const], writes=[pdb])
                prev_fin = make_finish(item, po, pob, ac, acb, pd, pdb)
                if h > 0 and (i + 1 == len(items) or items[i + 1][0] != h):
                    hd.pop(h - 1, None)
            if prev_fin is not None:
                prev_fin()
            while pp_late:
                pp_emit(pp_late.pop(0), ppf, ppo, "pool")
            P.barrier()
            ar.release(m)

        def build_G():
            m = ar.mark()
            rb = ar.alloc([33, 32], F32)
            oh = ar.alloc([33, OH_N], F32)
            gsb = ar.alloc([32, OH_N], BF16)
            b_rb, b_oh, b_g = Buf(), Buf(), Buf()
            P.op("pool", lambda e: e.memset(rb[32:33, :], -30000.0), writes=[b_rb])
            P.op("sp", lambda e: e.dma_start(out=rb[0:32, :], in_=relb_d[:, :]), writes=[b_rb], dma="g1")
            P.op("sp", lambda e: e.dma_start(out=oh[:, :], in_=oh_d[:, :]), writes=[b_oh], dma="g2")
            for (c0, n) in ((0, 512), (512, OH_N - 512)):
                bank, bb = PS[0], PSB[0]
                P.op("pe", lambda e, c0=c0, n=n: e.matmul(bank[0:32, 0:n], lhsT=rb[0:33, 0:32], rhs=oh[0:33, c0:c0 + n],
                                                          start=True, stop=True), reads=[b_rb, b_oh], writes=[bb])
                P.op("act", lambda e, c0=c0, n=n: e.activation(out=gsb[0:32, c0:c0 + n], in_=bank[0:32, 0:n],
                                                               func=AF.Exp), reads=[bb], writes=[b_g])
            b_gs = Buf()
            P.op("pool", lambda e: e.dma_start(out=Gs[:, :], in_=gsb[0:32, :]), reads=[b_g], writes=[b_gs], dma="g3")
            tR = ar.alloc([128, 32, 128], BF16)
            tF = ar.alloc([128, 32, 128], BF16)
            b_tR, b_tF = Buf(), Buf()

            def mk(dst, off, npart, nq):
                P.op("sp", lambda e: e.dma_start(out=tR[0:npart, :, 0:nq],
                                                 in_=AP(Gs.tensor, off, [[1, npart], [OH_N, 32], [1, nq]])),
                     reads=[b_gs], writes=[b_tR], dma="g4")
                P.op("dve", lambda e: e.tensor_copy(out=tF[0:npart, :, 0:nq],
                                                    in_=tR[0:npart, :, slice(nq - 1, None, -1)]),
                     reads=[b_tR], writes=[b_tF])
                P.op("pool", lambda e: e.dma_start(out=dst, in_=tF[0:npart, :, 0:nq]), reads=[b_tF], dma="g5")

            for d_ in range(3):
                mk(EBm_d[:, d_, :, :], OH_MAIN + 128 * d_, 128, 128)
            mk(EBk0_d[:, :, :], OH_K0, 16, 128)
            mk(EBkc_d[:, :, :], OH_KC, 16, 128)
            mk(EBq4_d[:, :, :], OH_Q4, 16, 16)
            mk(EBq5_d[:, :, :], OH_Q5, 128, 16)
            P.barrier()
            ar.release(m)

        def stage_SWA(seq, layer):
            j = layer // 2
            m = ar.mark()
            KT = cfg.ktiles()
            NKT = len(KT)
            kT = ar.alloc([128, 2, L], BF16)
            vx = ar.alloc([128, NKT, 4, 128], BF16)
            b_k = [Buf() for _ in range(NST)]
            b_v = [Buf() for _ in range(NST)]
            b_vones = Buf()
            P.op("pool", lambda e: e.memset(vx[:], 1.0), writes=[b_vones] + b_v)
            mkv, _ = cfg.mch[(layer, "kv")]
            mq, _ = cfg.mch[(layer, "q")]
            m1 = ar.mark()
            cx = common_ctx(2)
            xn = ar.alloc([128, KC, 528], BF16)
            xnb = [Buf(), Buf()]
            for s in range(NST):
                segs = cfg.segs(s)
                load_norm(cx, seq, layer == 0, s, gain_col(cfg, layer, "mix_pre"), xn, xnb)
                wa, _, wb = cx.ring.next()
                P.op("sp", lambda e, wa=wa: e.dma_start(
                    out=wa[:, 0:4, :, :].rearrange("p m k c -> p m (k c)"),
                    in_=Wb[mkv:mkv + 4].rearrange("m p e -> p m e")), reads=[b_wscr], writes=[wb], dma="L" + wb.name)
                for si, (col0, n) in enumerate(segs):
                    xo = 0 if si == 0 else 512
                    for pr_ in range(2):
                        bank, bb = cx.mbank.next()
                        for kc in range(KC):
                            P.op("pe", lambda e, wa=wa, pr_=pr_, kc=kc, bank=bank, xo=xo, n=n: e.matmul(
                                bank[:, 0:n], lhsT=wa[:, pr_, kc, :], rhs=xn[:, kc, xo:xo + n], start=(kc == 0),
                                stop=(kc == KC - 1)), reads=[wb, xnb[si]], writes=[bb])
                        evac(lambda pr_=pr_, col0=col0, n=n: kT[:, pr_, col0:col0 + n], bank, bb, n, [b_k[s]])
                    for t0 in range(0, n, 128):
                        nt = min(128, n - t0)
                        kt = (col0 + t0) // 128
                        bank, bb = cx.mbank.next()
                        for kc in range(KC):
                            P.op("pe", lambda e, wa=wa, kc=kc, bank=bank, xo=xo, t0=t0, nt=nt: e.matmul(
                                bank[0:nt, 0:256], lhsT=xn[:, kc, xo + t0:xo + t0 + nt], rhs=wa[:, 2:4, kc, :],
                                start=(kc == 0), stop=(kc == KC - 1)), reads=[wb, xnb[si]], writes=[bb])
                        P.op("dve", lambda e, bank=bank, kt=kt, nt=nt: e.tensor_copy(
                            out=vx[0:nt, kt, :, 0:64], in_=bank[0:nt, 0:256].rearrange("p (g d) -> p g d", g=4)),
                            reads=[bb], writes=[b_v[s]])
            P.barrier()
            ar.release(m1)
            cx = common_ctx(2)
            xo_buf = ar.alloc([128, KC, 528], BF16)
            qT = ar.alloc([128, KC, 528], BF16)
            EBm = ar.alloc([128, 3, 32, 128], BF16)
            EBk0 = ar.alloc([16, 32, 128], BF16)
            EBkc = ar.alloc([16, 32, 128], BF16)
            EBq4 = ar.alloc([16, 32, 16], BF16)
            EBq5 = ar.alloc([128, 32, 16], BF16)
            es32 = ar.alloc([1, 32], F32)
            esb = ar.alloc([1, 32], BF16)
            vsink = ar.alloc([1, 128], BF16)
            pT = Ring([(ar.alloc([128, 512], BF16), Buf()) for _ in range(4)])
            pT2 = Ring([(ar.alloc([128, 512], BF16), Buf()) for _ in range(8)])
            rden = Ring([(ar.alloc([64, 512], F32), Buf()) for _ in range(3)])
            b_eb, b_es = Buf(), Buf()
            P.op("sp", lambda e: e.dma_start(out=EBm[:, :, :, :], in_=EBm_d[:, :, :, :]), writes=[b_eb], dma="eb")
            P.op("sp", lambda e: e.dma_start(out=EBk0[:, :, :], in_=EBk0_d[:, :, :]), writes=[b_eb], dma="eb")
            P.op("sp", lambda e: e.dma_start(out=EBkc[:, :, :], in_=EBkc_d[:, :, :]), writes=[b_eb], dma="eb")
            P.op("sp", lambda e: e.dma_start(out=EBq4[:, :, :], in_=EBq4_d[:, :, :]), writes=[b_eb], dma="eb")
            P.op("sp", lambda e: e.dma_start(out=EBq5[:, :, :], in_=EBq5_d[:, :, :]), writes=[b_eb], dma="eb")
            P.op("sp", lambda e: e.dma_start(out=es32[0:1, :], in_=sink_d[j:j + 1, :]), writes=[b_es], dma="es")
            P.op("act", lambda e: e.activation(out=esb[0:1, :], in_=es32[0:1, :], func=AF.Exp), reads=[b_es],
                 writes=[b_es])
            P.op("pool", lambda e: e.memset(vsink[0:1, 0:64], 0.0), writes=[b_es])
            P.op("pool", lambda e: e.memset(vsink[0:1, 64:128], 1.0), writes=[b_es])
            sbank = bankring([0, 1, 2, 3, 4, 5])
            obank = bankring([6, 7])
            ebk = [0]
            ov = hview(oscr[seq])

            def esink_ap(h0, nq):
                a = esb[0:1, h0:h0 + 4]
                return AP(a.tensor, a.offset, [list(a.ap[0]), [1, 4], [0, nq]])

            class Unit:
                def __init__(u, s, xob, qb, qcol, nq, g, j0, keys):
                    u.s, u.xob, u.qb, u.qcol, u.nq, u.g, u.j0, u.keys = s, xob, qb, qcol, nq, g, j0, keys
                    u.p_, u.half = divmod(g, 2)
                    u.hb0 = u.half * 64
                    u.c0 = u.p_ * 8 + j0
                    u.h0 = 8 * g + j0
                    u.N = 4 * nq
                    u.sts = []

                def A(u):
                    hb0, p_, c0, qcol, nq, N = u.hb0, u.p_, u.c0, u.qcol, u.nq, u.N
                    for ki, (kt, nk, ebf) in enumerate(u.keys):
                        k0 = KT[kt][0]
                        sk = min(k0 // 512, NST - 1)
                        sb_, sbb = sbank.next()
                        P.op("pe", lambda e, sb_=sb_, k0=k0, nk=nk: e.matmul(
                            sb_[0:nk, 0:N], lhsT=kT[hb0:hb0 + 64, p_, k0:k0 + nk],
                            rhs=qT[hb0:hb0 + 64, c0:c0 + 4, qcol:qcol + nq], start=True, stop=True),
                            reads=[b_k[sk], u.qb], writes=[sbb])
                        u.sts.append((sb_, sbb))

                def B(u):
                    N, h0 = u.N, u.h0
                    u.p2 = []
                    for ki, (kt, nk, ebf) in enumerate(u.keys):
                        sb_, sbb = u.sts[ki]
                        pt, ptb = pT.next()
                        P.op("act", lambda e, pt=pt, sb_=sb_, nk=nk: e.activation(
                            out=pt[0:nk, 0:N], in_=sb_[0:nk, 0:N], func=AF.Exp, scale=0.125), reads=[sbb],
                            writes=[ptb])
                        pt2, pt2b = pT2.next()
                        ebk[0] += 1
                        eng = "pool" if ebk[0] % 6 == 0 else "dve"
                        P.op(eng, lambda e, pt=pt, pt2=pt2, nk=nk, ebf=ebf: e.tensor_tensor(
                            out=pt2[0:nk, 0:N], in0=pt[0:nk, 0:N], in1=ebf(h0).rearrange("p h q -> p (h q)"),
                            op=ALU.mult), reads=[ptb, b_eb], writes=[pt2b])
                        u.p2.append((pt2, pt2b))

                def C(u):
                    N, h0, g, nq = u.N, u.h0, u.g, u.nq
                    po, pob = obank.next()
                    u.po, u.pob = po, pob
                    order = sorted(range(len(u.keys)), key=lambda ki: (u.keys[ki][1] < 128))
                    for oi, ki in enumerate(order):
                        kt, nk, ebf = u.keys[ki]
                        k0 = KT[kt][0]
                        sk = min(k0 // 512, NST - 1)
                        pt2, pt2b = u.p2[ki]
                        P.op("pe", lambda e, pt2=pt2, kt=kt, nk=nk, oi=oi: e.matmul(
                            po[:, 0:N], lhsT=vx[0:nk, kt, g, :], rhs=pt2[0:nk, 0:N], start=(oi == 0), stop=False),
                            reads=[b_v[sk], b_vones, pt2b], writes=[pob])
                    P.op("pe", lambda e: e.matmul(po[:, 0:N], lhsT=vsink[0:1, :], rhs=esink_ap(h0, nq),
                                                  start=False, stop=True), reads=[b_es], writes=[pob])

                def Dk(u):
                    N, hb0, c0, qcol, nq = u.N, u.hb0, u.c0, u.qcol, u.nq
                    po, pob = u.po, u.pob
                    rd, rdb = rden.next()
                    P.op("act", lambda e: e.activation(out=rd[0:64, 0:N], in_=po[64:128, 0:N], func=AF.Ln),
                         reads=[pob], writes=[rdb])
                    P.op("act", lambda e: e.activation(out=rd[0:64, 0:N], in_=rd[0:64, 0:N], func=AF.Exp, scale=-1.0),
                         reads=[rdb], writes=[rdb])
                    P.op("dve", lambda e: e.tensor_tensor(
                        out=xo_buf[hb0:hb0 + 64, c0:c0 + 4, qcol:qcol + nq],
                        in0=po[0:64, 0:N].rearrange("p (h q) -> p h q", h=4),
                        in1=rd[0:64, 0:N].rearrange("p (h q) -> p h q", h=4), op=ALU.mult),
                        reads=[pob, rdb], writes=[u.xob])

            def run_units(units):
                n = len(units)
                if n == 0:
                    return
                units[0].A()
                for i in range(n):
                    units[i].B()
                    if i + 1 < n:
                        units[i + 1].A()
                    units[i].C()
                    if i > 0:
                        units[i - 1].Dk()
                units[n - 1].Dk()

            xnb = [Buf(), Buf()]
            qb = [Buf(), Buf()]
            for s in range(NST):
                segs = cfg.segs(s)
                load_norm(cx, seq, False, s, gain_col(cfg, layer, "mix_pre"), xo_buf, xnb)

                def xin_fn(si, kc, n):
                    xo = 0 if si == 0 else 512
                    return xo_buf[:, kc, xo:xo + n]

                def epi(mi, si, n, bank, bb, qb=qb):
                    xo = 0 if si == 0 else 512
                    evac(lambda: qT[:, mi, xo:xo + n], bank, bb, n, [qb[si]])

                proj2048(cx, mq, 16, xin_fn, xnb, segs, epi)
                units = []
                for bi in range(4):
                    b = 4 * s + bi
                    for g in range(4):
                        for j0 in (0, 4):
                            keys = []
                            keys.append((NKT - 1, 16, (lambda h0, b=b: (EBk0 if b == 0 else EBkc)[0:16, h0:h0 + 4, :])))
                            if b - 1 >= 0:
                                keys.append((b - 1, 128, (lambda h0: EBm[:, 0, h0:h0 + 4, :])))
                            keys.append((b, 128, (lambda h0: EBm[:, 1, h0:h0 + 4, :])))
                            if b + 1 < NB:
                                keys.append((b + 1, 128, (lambda h0: EBm[:, 2, h0:h0 + 4, :])))
                            units.append(Unit(s, xnb[0], qb[0], 128 * bi, 128, g, j0, keys))
                if len(segs) > 1:
                    for g in range(4):
                        for j0 in (0, 4):
                            keys = [(NKT - 1, 16, (lambda h0: EBq4[0:16, h0:h0 + 4, :])),
                                    (0, 128, (lambda h0: EBq5[:, h0:h0 + 4, :]))]
                            units.append(Unit(s, xnb[1], qb[1], 512, 16, g, j0, keys))
                run_units(units)
                for si, (col0, n) in enumerate(segs):
                    xo = 0 if si == 0 else 512
                    P.op("pool", lambda e, col0=col0, n=n, xo=xo: e.dma_start(
                        out=ov[:, :, col0:col0 + n], in_=xo_buf[:, :, xo:xo + n]), reads=[xnb[si]],
                        writes=[obufs[seq][s]], dma="Sxo%d" % si)
            P.barrier()
            ar.release(m)

        prepass()
        if cfg.nswa > 0:
            build_G()
        for seq in range(NSEQ):
            for layer in range(cfg.depth):
                if layer % 2 == 0:
                    stage_MLA(seq, layer)
                else:
                    stage_SWA(seq, layer)
                stage_C(seq, layer)
                stage_D(seq, layer)
        P.finalize()
        P.emit(nc)
    return nc, P


def run_cfg(cfg, xs_per_core, w, core_ids):
    W, wout, wuq, wukv, gains = prep_weights(cfg, w)
    nc, P = build_program(cfg)
    shared = {
        "metaT": np.ascontiguousarray(np.asarray(w["meta_tokens"], np.float32).T),
        "W2048": W.reshape(cfg.NM, 128, KC * 128),
        "Wout": wout.reshape(cfg.depth * 16, 128, 64 * 128),
        "Wuq": wuq.reshape(cfg.nmla * 16, 128, 1024),
        "Wukv": wukv.reshape(cfg.nmla * 16, 128, 1024),
        "gains": gains,
        "rel_bias": np.ascontiguousarray(np.asarray(w["rel_bias"], np.float32)),
        "sink": np.ascontiguousarray(np.asarray(w["swa_sink"], np.float32)[:max(cfg.nswa, 1)]),
        "cs_tab": build_cs(cfg),
        "onehot": build_onehot(),
    }
    in_maps = []
    for xs in xs_per_core:
        d = dict(shared)
        d["xT"] = np.ascontiguousarray(np.stack([np.asarray(x, np.float32).T for x in xs]))
        in_maps.append(d)
    res = run_bass_kernel_spmd(nc, in_maps, core_ids=core_ids)
    return res


def kernel(**inputs):
    cfg = Cfg(S=4096, depth=4, nseq=2)
    xp = np.asarray(inputs["x_prompt"], np.float32)
    xs = np.asarray(inputs["x_sample"], np.float32)
    seqs = [("p", i) for i in range(4)] + [("s", i) for i in range(8)]

    def get(t):
        return xp[t[1]] if t[0] == "p" else xs[t[1]]

    slot_seq = seqs + [("s", 4), ("s", 5), ("s", 6), ("s", 7)]
    per_core = [[get(slot_seq[c]), get(slot_seq[8 + c])] for c in range(NCORES)]
    w = {k: np.asarray(v) for k, v in inputs.items() if k not in ("x_prompt", "x_sample")}
    res = run_cfg(cfg, per_core, w, list(range(NCORES)))
    yp = np.empty_like(xp)
    ys = np.empty_like(xs)
    for c in range(NCORES):
        y = res.results[c]["yT"]
        for q, slot in enumerate((c, 8 + c)):
            if slot >= 12:
                continue
            t = slot_seq[slot]
            (yp if t[0] == "p" else ys)[t[1]] = y[q].T
    return (yp, ys)
```

```python
import math
import contextlib
import numpy as np
import ml_dtypes
import concourse.bass as bass
import concourse.mybir as mybir
from concourse.ap import AP
from concourse.bass_utils import run_bass_kernel_spmd

F32 = mybir.dt.float32
BF16 = mybir.dt.bfloat16
AF = mybir.ActivationFunctionType
ALU = mybir.AluOpType

D = 2048
KC = 16
NMETA = 16
DFF = 8192
EPS = 1e-6
NCORES = 8


class Buf:
    __slots__ = ("name", "w", "r", "const")

    def __init__(self, name="", const=False):
        self.name = name
        self.w = None
        self.r = []
        self.const = const


class Op:
    __slots__ = ("eng", "fn", "deps", "sig", "sem", "val", "isdma", "idx")


class Prog:
    ENGS = ("pe", "act", "dve", "pool", "sp")

    def __init__(self):
        self.ops = []
        self.last = {}
        self.lastdma = {}

    def op(self, eng, fn, reads=(), writes=(), dma=None):
        o = Op()
        o.eng = eng
        o.fn = fn
        o.deps = set()
        o.sig = False
        o.sem = dma
        o.val = 0
        o.isdma = dma is not None
        o.idx = len(self.ops)
        deps = o.deps
        isdma = o.isdma
        for b in reads:
            if b.w is not None:
                deps.add(b.w)
        for b in writes:
            x = b.w
            if x is not None and (isdma or x.isdma or x.eng != eng):
                deps.add(x)
            for x in b.r:
                if isdma or x.isdma or x.eng != eng:
                    deps.add(x)
        for b in reads:
            if not b.const:
                if not isdma and b.r:
                    b.r = [x for x in b.r if x.isdma or x.eng != eng]
                b.r.append(o)
        for b in writes:
            b.w = o
            b.r = []
        self.ops.append(o)
        if o.isdma:
            self.lastdma[dma] = o
        else:
            self.last[eng] = o
        return o

    def barrier(self):
        lasts = list(self.last.values()) + list(self.lastdma.values())
        for e in self.ENGS:
            o = Op()
            o.eng = e
            o.fn = None
            o.deps = set(lasts)
            o.sig = False
            o.sem = None
            o.val = 0
            o.isdma = False
            o.idx = len(self.ops)
            self.ops.append(o)

    def finalize(self):
        for o in self.ops:
            nd = set()
            for d in o.deps:
                if d is o:
                    continue
                if (not d.isdma) and (not o.isdma) and d.eng == o.eng and o.eng == "pe":
                    continue
                nd.add(d)
            o.deps = nd
            for d in nd:
                d.sig = True
        cnt = {e: 0 for e in self.ENGS}
        dcnt = {}
        for o in self.ops:
            if o.isdma:
                dcnt[o.sem] = dcnt.get(o.sem, 0) + 16
                o.val = dcnt[o.sem]
            elif o.sig:
                cnt[o.eng] += 1
                o.val = cnt[o.eng]
        self.dma_keys = list(dcnt.keys())
        self.stats = (len(self.ops), dict(cnt), len(dcnt))

    def emit(self, nc):
        with contextlib.ExitStack() as st:
            sems = {}
            for e in self.ENGS:
                sems[e] = st.enter_context(nc.semaphore("s_" + e))
            for k in self.dma_keys:
                sems[("dma", k)] = st.enter_context(nc.semaphore("d_" + str(k)))
            block = st.enter_context(nc.Block())
            ops = self.ops

            def run(engname):
                def body(eng):
                    seen = {}
                    for o in ops:
                        if o.eng != engname:
                            continue
                        if o.deps:
                            for d in sorted(o.deps, key=lambda x: x.idx):
                                key = ("dma", d.sem) if d.isdma else d.eng
                                if seen.get(key, 0) >= d.val:
                                    continue
                                seen[key] = d.val
                                eng.wait_ge(sems[key], d.val)
                        if o.fn is None:
                            continue
                        ins = o.fn(eng)
                        if o.isdma:
                            ins.then_inc(sems[("dma", o.sem)], 16)
                        elif o.sig:
                            ins.then_inc(sems[o.eng], 1)
                    if engname == "sp":
                        for k, o in self.lastdma.items():
                            if seen.get(("dma", k), 0) < o.val:
                                eng.wait_ge(sems[("dma", k)], o.val)
                return body

            block.tensor(run("pe"))
            block.scalar(run("act"))
            block.vector(run("dve"))
            block.gpsimd(run("pool"))
            block.sync(run("sp"))


class Cfg:
    def __init__(self, S=4096, depth=4, nseq=2, dbg=False):
        self.S = S
        self.L = S + NMETA
        self.depth = depth
        self.nseq = nseq
        self.NB = S // 128
        self.NST = S // 512
        self.dbg = dbg
        assert S % 512 == 0
        self.mch = {}
        n = 0
        for i in range(depth):
            if i % 2 == 0:
                for nm, c in (("dq", 4), ("dkv", 5), ("wo", 16)):
                    self.mch[(i, nm)] = (n, c)
                    n += c
            else:
                for nm, c in (("kv", 4), ("q", 16), ("wo", 16)):
                    self.mch[(i, nm)] = (n, c)
                    n += c
            self.mch[(i, "win")] = (n, 64)
            n += 64
        self.NM = n
        self.nmla = (depth + 1) // 2
        self.nswa = depth // 2

    def segs(self, s):
        r = [(512 * s, 512)]
        if s == self.NST - 1:
            r.append((self.S, NMETA))
        return r

    def ktiles(self):
        r = [(128 * b, 128) for b in range(self.NB)]
        r.append((self.S, NMETA))
        return r


def t5_bucket_np(rel):
    nb = 16
    me = 8
    rel = np.asarray(rel)
    ret = np.where(rel > 0, nb, 0)
    n = np.abs(rel)
    nf = np.maximum(n, 1).astype(np.float32)
    large = me + (np.log(nf / np.float32(me)) / np.float32(math.log(128 / me)) * np.float32(nb - me)).astype(np.int32)
    large = np.minimum(large, nb - 1)
    return ret + np.where(n < me, n, large)


OH_MAIN = 0
OH_K0 = 512
OH_KC = 656
OH_Q4 = 800
OH_Q5 = 832
OH_N = 976


def build_onehot():
    oh = np.zeros((33, OH_N), np.float32)

    def fill(base, n, relf, validf):
        for j in range(n):
            rel = relf(j)
            if validf(rel):
                oh[int(t5_bucket_np(rel)), base + j] = 1.0
            else:
                oh[32, base + j] = 1.0

    fill(OH_MAIN, 512, lambda i: i - 255, lambda r: abs(r) <= 128)
    fill(OH_K0, 144, lambda i: i - 143, lambda r: True)
    fill(OH_KC, 144, lambda i: i - 143 - 128, lambda r: True)
    fill(OH_Q4, 32, lambda i: i - 15, lambda r: True)
    fill(OH_Q5, 144, lambda i: i + 1, lambda r: r <= 128)
    return oh


def build_cs(cfg):
    S, L = cfg.S, cfg.L
    pos = np.concatenate([np.arange(S) + NMETA, np.arange(NMETA)]).astype(np.float32)
    half = 32
    inv = (np.float32(10000.0) ** (-(np.arange(half, dtype=np.float32) / np.float32(half)))).astype(np.float32)
    ang = (pos[None, :] * inv[:, None]).astype(np.float32)
    c = np.cos(ang).astype(np.float32)
    s = np.sin(ang).astype(np.float32)
    return np.concatenate([c, c, -s, s], axis=0).astype(np.float32)


def mchunk(src, cols, rows=None):
    a = src[:, cols]
    if rows is not None:
        a = a[rows, :]
    return np.ascontiguousarray(a.reshape(KC, 128, 128).transpose(1, 0, 2))


def swa_q_cols(c):
    p, j = divmod(c, 8)
    h0 = 8 * (2 * p) + j
    h1 = 8 * (2 * p + 1) + j
    return np.concatenate([np.arange(h0 * 64, h0 * 64 + 64), np.arange(h1 * 64, h1 * 64 + 64)])


def prep_weights(cfg, w):
    W = np.empty((cfg.NM, 128, KC, 128), np.float32)
    for i in range(cfg.depth):
        j = i // 2
        if i % 2 == 0:
            m0, _ = cfg.mch[(i, "dq")]
            for k in range(4):
                W[m0 + k] = mchunk(w["mla_w_dq"][j], np.arange(128 * k, 128 * k + 128))
            m0, _ = cfg.mch[(i, "dkv")]
            for k in range(4):
                W[m0 + k] = mchunk(w["mla_w_dkv"][j], np.arange(128 * k, 128 * k + 128))
            rc = np.concatenate([np.arange(512, 576), np.arange(544, 576), np.arange(512, 544)])
            W[m0 + 4] = mchunk(w["mla_w_dkv"][j], rc)
            m0, _ = cfg.mch[(i, "wo")]
            for k in range(16):
                W[m0 + k] = mchunk(w["mla_w_o"][j], np.arange(128 * k, 128 * k + 128))
        else:
            m0, _ = cfg.mch[(i, "kv")]
            for k in range(4):
                W[m0 + k] = mchunk(w["swa_w_qkv"][j], np.arange(2048 + 128 * k, 2048 + 128 * k + 128))
            m0, _ = cfg.mch[(i, "q")]
            for k in range(16):
                W[m0 + k] = mchunk(w["swa_w_qkv"][j], swa_q_cols(k))
            m0, _ = cfg.mch[(i, "wo")]
            rows = np.concatenate([swa_q_cols(k) for k in range(16)])
            for k in range(16):
                W[m0 + k] = mchunk(w["swa_w_o"][j], np.arange(128 * k, 128 * k + 128), rows=rows)
        m0, _ = cfg.mch[(i, "win")]
        win = w["mlp_w_in"][i]
        W[m0:m0 + 64] = win.reshape(KC, 128, 64, 128).transpose(2, 1, 0, 3)
    wout = np.ascontiguousarray(
        np.asarray(w["mlp_w_out"][:cfg.depth]).reshape(cfg.depth, 64, 128, 16, 128).transpose(0, 3, 2, 1, 4))
    nm = cfg.nmla
    wuq = np.empty((nm, 16, 128, 4, 256), np.float32)
    wukv = np.empty((nm, 16, 128, 4, 256), np.float32)
    for j in range(nm):
        for h in range(16):
            b = h * 192
            cols = np.concatenate([np.arange(b, b + 192), np.arange(b + 160, b + 192), np.arange(b + 128, b + 160)])
            wuq[j, h] = w["mla_w_uq"][j][:, cols].reshape(4, 128, 256).transpose(1, 0, 2)
            wukv[j, h] = w["mla_w_ukv"][j][:, h * 256:(h + 1) * 256].reshape(4, 128, 256).transpose(1, 0, 2)
    cols = []
    for i in range(cfg.depth):
        for nmn in ("norm_mix_pre", "norm_mix_post", "norm_mlp_pre", "norm_mlp_post"):
            cols.append(np.asarray(w[nmn][i]).reshape(16, 128).T)
    for j in range(nm):
        cols.append(np.asarray(w["mla_q_norm"][j]).reshape(4, 128).T)
        cols.append(np.asarray(w["mla_kv_norm"][j]).reshape(4, 128).T)
    gains = np.ascontiguousarray(np.concatenate(cols, axis=1).astype(np.float32))
    return W, wout, wuq, wukv, gains


def gain_col(cfg, i, kind):
    return i * 64 + {"mix_pre": 0, "mix_post": 16, "mlp_pre": 32, "mlp_post": 48}[kind]


def lat_gain_col(cfg, j, kind):
    return cfg.depth * 64 + j * 8 + (0 if kind == "q" else 4)


class Arena:
    def __init__(self, nc, base, top):
        self.nc = nc
        self.base = base
        self.off = base
        self.top = top
        self.n = 0

    def alloc(self, shape, dt):
        nbytes = int(np.prod(shape[1:])) * (4 if dt == F32 else 2)
        nbytes = (nbytes + 63) // 64 * 64
        assert self.off + nbytes <= self.top, ("SBUF overflow", self.off, nbytes, self.top)
        self.n += 1
        h = self.nc.alloc_sbuf_tensor_at("t%d" % self.n, list(shape), dt, offset=self.off)
        self.off += nbytes
        return h

    def mark(self):
        return self.off

    def release(self, m):
        self.off = m


class Ring:
    def __init__(self, items):
        self.items = items
        self.i = 0

    def next(self):
        it = self.items[self.i % len(self.items)]
        self.i += 1
        return it


def build_program(cfg):
    nc = bass.Bass("TRN2", target_bir_lowering=False)
    S, L, NST, NB = cfg.S, cfg.L, cfg.NST, cfg.NB
    NSEQ = cfg.nseq
    P = Prog()

    def din(name, shape, dt=F32):
        return nc.dram_tensor(name, list(shape), dt, kind="ExternalInput").ap()

    def dscr(name, shape, dt):
        return nc.dram_tensor(name, list(shape), dt, kind="Internal").ap()

    xT = din("xT", [NSEQ, D, S])
    metaT = din("metaT", [D, NMETA])
    Wf = din("W2048", [cfg.NM, 128, KC * 128])
    Woutf = din("Wout", [cfg.depth * 16, 128, 64 * 128])
    Wuqf = din("Wuq", [cfg.nmla * 16, 128, 1024])
    Wukvf = din("Wukv", [cfg.nmla * 16, 128, 1024])
    gains_d = din("gains", [128, cfg.depth * 64 + cfg.nmla * 8])
    relb_d = din("rel_bias", [32, 32])
    sink_d = din("sink", [max(cfg.nswa, 1), 32])
    cs_d = din("cs_tab", [128, L])
    oh_d = din("onehot", [33, OH_N])
    yT = nc.dram_tensor("yT", [NSEQ, D, S], F32, kind="ExternalOutput").ap()

    Wb = dscr("W2048b", [cfg.NM, 128, KC * 128], BF16)
    Woutb = dscr("Woutb", [cfg.depth * 16, 128, 64 * 128], BF16)
    Wuqb = dscr("Wuqb", [cfg.nmla * 16, 128, 1024], BF16)
    Wukvb = dscr("Wukvb", [cfg.nmla * 16, 128, 1024], BF16)
    if cfg.dbg:
        hscr = nc.dram_tensor("hscr", [NSEQ, D, L], F32, kind="ExternalOutput").ap()
    else:
        hscr = dscr("hscr", [NSEQ, D, L], F32)
    oscr = dscr("oscr", [NSEQ, D, L], BF16)
    Gs = dscr("Gs", [32, OH_N], BF16)
    EBm_d = dscr("EBm_d", [128, 3, 32, 128], BF16)
    EBk0_d = dscr("EBk0_d", [16, 32, 128], BF16)
    EBkc_d = dscr("EBkc_d", [16, 32, 128], BF16)
    EBq4_d = dscr("EBq4_d", [16, 32, 16], BF16)
    EBq5_d = dscr("EBq5_d", [128, 32, 16], BF16)

    NGAIN = cfg.depth * 64 + cfg.nmla * 8

    with contextlib.ExitStack() as top:
        ar = Arena(nc, 18432, nc.sbuf_top - 64)
        PS = [top.enter_context(nc.psum_tensor("ps%d" % i, [128, 512], F32)) for i in range(8)]
        PSB = [Buf("ps%d" % i) for i in range(8)]

        def bankring(ids):
            return Ring([(PS[i], PSB[i]) for i in ids])

        ones_bf = ar.alloc([128, 128], BF16)
        ones_f = ar.alloc([128, 128], F32)
        gains = ar.alloc([128, NGAIN], F32)
        b_const = Buf("const")
        P.op("pool", lambda e: e.memset(ones_bf[:], 1.0), writes=[b_const])
        P.op("pool", lambda e: e.memset(ones_f[:], 1.0), writes=[b_const])
        P.op("sp", lambda e: e.dma_start(out=gains[:], in_=gains_d[:, :]), writes=[b_const], dma="const")
        P.barrier()
        b_const.const = True
        stage_base = ar.mark()

        hbufs = [[[Buf("h%d_%d_%d" % (q, s, c)) for c in range(4)] for s in range(NST)] for q in range(NSEQ)]
        obufs = [[Buf("o%d_%d" % (q, s)) for s in range(NST)] for q in range(NSEQ)]
        b_wscr = Buf("wscr")

        def hview(ap2d):
            return ap2d.rearrange("(c p) l -> p c l", p=128)

        def pp_unit_list():
            early, late = [], []

            def w2048(i0, cnt, lst):
                i = i0
                while i < i0 + cnt:
                    k = min(2, i0 + cnt - i)
                    lst.append((Wf[i:i + k].rearrange("m p e -> p m e"), Wb[i:i + k].rearrange("m p e -> p m e"),
                                k, 2048))
                    i += k

            def small(srcT, dstT, i0, cnt, lst):
                i = i0
                while i < i0 + cnt:
                    k = min(4, i0 + cnt - i)
                    lst.append((srcT[i:i + k].rearrange("m p e -> p m e"), dstT[i:i + k].rearrange("m p e -> p m e"),
                                k, 1024))
                    i += k

            def wout(layer, lst):
                for dmc in range(16):
                    i = layer * 16 + dmc
                    for hf in range(2):
                        lst.append((Woutf[i:i + 1, :, hf * 4096:(hf + 1) * 4096].rearrange("m p e -> p m e"),
                                    Woutb[i:i + 1, :, hf * 4096:(hf + 1) * 4096].rearrange("m p e -> p m e"), 1, 4096))

            for layer in range(cfg.depth):
                jj = layer // 2
                if layer % 2 == 0:
                    lst = early if layer == 0 else late
                    m0, c = cfg.mch[(layer, "dq")]
                    w2048(m0, c, lst)
                    m0, c = cfg.mch[(layer, "dkv")]
                    w2048(m0, c, lst)
                    small(Wuqf, Wuqb, jj * 16, 16, lst)
                    small(Wukvf, Wukvb, jj * 16, 16, lst)
                else:
                    m0, c = cfg.mch[(layer, "kv")]
                    w2048(m0, c, late)
                    m0, c = cfg.mch[(layer, "q")]
                    w2048(m0, c, late)
                m0, c = cfg.mch[(layer, "wo")]
                w2048(m0, c, late)
                m0, c = cfg.mch[(layer, "win")]
                w2048(m0, c, late)
                wout(layer, late)
            return early, late

        pp_late = []

        def pp_alloc():
            fin = Ring([(ar.alloc([128, 4096], F32), Buf("pi%d" % i)) for i in range(2)])
            fout = Ring([(ar.alloc([128, 4096], BF16), Buf("po%d" % i)) for i in range(2)])
            return fin, fout

        def pp_emit(unit, fin, fout, eng):
            src, dst, a, b = unit
            fi, bi = fin.next()
            fo, bo = fout.next()
            ne = a * b
            P.op("sp", lambda e: e.dma_start(out=fi[:, 0:ne].rearrange("p (m e) -> p m e", m=a), in_=src),
                 writes=[bi], dma="L" + bi.name)
            if eng == "act":
                P.op("act", lambda e: e.copy(out=fo[:, 0:ne], in_=fi[:, 0:ne]), reads=[bi], writes=[bo])
            else:
                P.op(eng, lambda e: e.tensor_copy(out=fo[:, 0:ne], in_=fi[:, 0:ne]), reads=[bi], writes=[bo])
            P.op("pool", lambda e: e.dma_start(out=dst, in_=fo[:, 0:ne].rearrange("p (m e) -> p m e", m=a)),
                 reads=[bo], dma="S" + bo.name)

        def prepass():
            m = ar.mark()
            early, late = pp_unit_list()
            pp_late.extend(late)
            fin, fout = pp_alloc()
            engs = ["dve", "act", "pool"]
            for i, u in enumerate(early):
                pp_emit(u, fin, fout, engs[i % 3])
            P.barrier()
            b_wscr.const = True
            ar.release(m)

        class Ctx:
            pass

        def alloc_ring(nslots):
            items = []
            for _ in range(nslots):
                a = ar.alloc([128, 4, KC, 128], BF16)
                off = ar.off - 16384
                ar.n += 1
                b = nc.alloc_sbuf_tensor_at("t%d" % ar.n, [128, 64, 128], BF16, offset=off)
                items.append((a, b, Buf("w%d" % len(items))))
            return Ring(items)

        dmak = [0]

        def dkey(prefix, n=4):
            dmak[0] += 1
            return "%s%d" % (prefix, dmak[0] % n)

        def src_h(seq, use_x):
            if use_x:
                xv = hview(xT[seq])
                mv = hview(metaT)

                def f(col0, n, c0, ncn):
                    if col0 >= S:
                        return mv[:, c0:c0 + ncn, col0 - S:col0 - S + n]
                    return xv[:, c0:c0 + ncn, col0:col0 + n]
                return f
            hv = hview(hscr[seq])
            return lambda col0, n, c0, ncn: hv[:, c0:c0 + ncn, col0:col0 + n]

        def dst_h(seq, last):
            if last:
                yv = hview(yT[seq])

                def f(col0, n, c0, ncn):
                    if col0 >= S:
                        return None
                    return yv[:, c0:c0 + ncn, col0:col0 + n]
                return f
            hv = hview(hscr[seq])
            return lambda col0, n, c0, ncn: hv[:, c0:c0 + ncn, col0:col0 + n]

        def norm_stats(cx, srcs, n, nchunks, inv_dim):
            bank, bb = cx.nbank.next()
            for c in range(nchunks):
                apf, sb = srcs[c]
                sq, sqb = cx.sq.next()
                P.op("act", lambda e, apf=apf, sq=sq: e.activation(out=sq[:, 0, 0:n], in_=apf(), func=AF.Square),
                     reads=[sb], writes=[sqb])
                P.op("pe", lambda e, sq=sq, bank=bank, c=c: e.matmul(
                    bank[:, 0:n], lhsT=ones_bf[:, :], rhs=sq[:, 0, 0:n], start=(c == 0), stop=(c == nchunks - 1)),
                    reads=[sqb, b_const], writes=[bb])
            rs, rsb = cx.rstd.next()
            P.op("act", lambda e: e.activation(out=rs[:, 0:n], in_=bank[:, 0:n], func=AF.Ln, scale=inv_dim, bias=EPS),
                 reads=[bb], writes=[rsb])
            P.op("act", lambda e: e.activation(out=rs[:, 0:n], in_=rs[:, 0:n], func=AF.Exp, scale=-0.5),
                 reads=[rsb], writes=[rsb])
            return rs, rsb

        def load_norm(cx, seq, use_x, s, gcol, xn, xnb):
            sf = src_h(seq, use_x)
            hbq = hbufs[seq][s]
            for si, (col0, n) in enumerate(cfg.segs(s)):
                xo = 0 if si == 0 else 512
                bank, bb = cx.nbank.next()
                for q in range(4):
                    hq, hqb = cx.hq.next()
                    P.op("sp", lambda e, hq=hq, q=q, col0=col0, n=n: e.dma_start(
                        out=hq[:, :, 0:n], in_=sf(col0, n, 4 * q, 4)), reads=[hbq[q]], writes=[hqb], dma="L" + hqb.name)
                    sq, sqb = cx.sq.next()
                    P.op("act", lambda e, hq=hq, sq=sq, n=n: e.activation(out=sq[:, :, 0:n], in_=hq[:, :, 0:n],
                                                                         func=AF.Square), reads=[hqb], writes=[sqb])
                    for c in range(4):
                        P.op("pe", lambda e, sq=sq, bank=bank, c=c, q=q, n=n: e.matmul(
                            bank[:, 0:n], lhsT=ones_bf[:, :], rhs=sq[:, c, 0:n], start=(q == 0 and c == 0),
                            stop=(q == 3 and c == 3)), reads=[sqb, b_const], writes=[bb])
                rs, rsb = cx.rstd.next()
                P.op("act", lambda e, rs=rs, bank=bank, n=n: e.activation(
                    out=rs[:, 0:n], in_=bank[:, 0:n], func=AF.Ln, scale=1.0 / D, bias=EPS), reads=[bb], writes=[rsb])
                P.op("act", lambda e, rs=rs, n=n: e.activation(out=rs[:, 0:n], in_=rs[:, 0:n], func=AF.Exp,
                                                               scale=-0.5), reads=[rsb], writes=[rsb])
                for q in range(4):
                    hq, hqb = cx.hq.next()
                    P.op("sp", lambda e, hq=hq, q=q, col0=col0, n=n: e.dma_start(
                        out=hq[:, :, 0:n], in_=sf(col0, n, 4 * q, 4)), reads=[hbq[q]], writes=[hqb], dma="L" + hqb.name)
                    for c in range(4):
                        cc = 4 * q + c
                        P.op("dve", lambda e, hq=hq, c=c, cc=cc, rs=rs, n=n, xo=xo: e.scalar_tensor_tensor(
                            out=xn[:, cc, xo:xo + n], in0=hq[:, c, 0:n], scalar=gains[:, gcol + cc:gcol + cc + 1],
                            in1=rs[:, 0:n], op0=ALU.mult, op1=ALU.mult), reads=[hqb, rsb, b_const], writes=[xnb[si]])

        def load_norm_a(cx, seq, use_x, s):
            sf = src_h(seq, use_x)
            hbq = hbufs[seq][s]
            st = []
            for si, (col0, n) in enumerate(cfg.segs(s)):
                sqs = []
                for q in range(4):
                    hq, hqb = cx.hq.next()
                    P.op("sp", lambda e, hq=hq, q=q, col0=col0, n=n: e.dma_start(
                        out=hq[:, :, 0:n], in_=sf(col0, n, 4 * q, 4)), reads=[hbq[q]], writes=[hqb], dma="L" + hqb.name)
                    sq, sqb = cx.sq.next()
                    P.op("act", lambda e, hq=hq, sq=sq, n=n: e.activation(out=sq[:, :, 0:n], in_=hq[:, :, 0:n],
                                                                         func=AF.Square), reads=[hqb], writes=[sqb])
                    sqs.append((sq, sqb))
                st.append(sqs)
            return st

        def load_norm_b(cx, seq, use_x, s, gcol, xn, xnb, st):
            sf = src_h(seq, use_x)
            hbq = hbufs[seq][s]
            for si, (col0, n) in enumerate(cfg.segs(s)):
                xo = 0 if si == 0 else 512
                bank, bb = cx.nbank.next()
                for q in range(4):
                    sq, sqb = st[si][q]
                    for c in range(4):
                        P.op("pe", lambda e, sq=sq, bank=bank, c=c, q=q, n=n: e.matmul(
                            bank[:, 0:n], lhsT=ones_bf[:, :], rhs=sq[:, c, 0:n], start=(q == 0 and c == 0),
                            stop=(q == 3 and c == 3)), reads=[sqb, b_const], writes=[bb])
                rs, rsb = cx.rstd.next()
                P.op("act", lambda e, rs=rs, bank=bank, n=n: e.activation(
                    out=rs[:, 0:n], in_=bank[:, 0:n], func=AF.Ln, scale=1.0 / D, bias=EPS), reads=[bb], writes=[rsb])
                P.op("act", lambda e, rs=rs, n=n: e.activation(out=rs[:, 0:n], in_=rs[:, 0:n], func=AF.Exp,
                                                               scale=-0.5), reads=[rsb], writes=[rsb])
                for q in range(4):
                    hq, hqb = cx.hq.next()
                    P.op("pool", lambda e, hq=hq, q=q, col0=col0, n=n: e.dma_start(
                        out=hq[:, :, 0:n], in_=sf(col0, n, 4 * q, 4)), reads=[hbq[q]], writes=[hqb], dma="L" + hqb.name)
                    for c in range(4):
                        cc = 4 * q + c
                        P.op("dve", lambda e, hq=hq, c=c, cc=cc, rs=rs, n=n, xo=xo: e.scalar_tensor_tensor(
                            out=xn[:, cc, xo:xo + n], in0=hq[:, c, 0:n], scalar=gains[:, gcol + cc:gcol + cc + 1],
                            in1=rs[:, 0:n], op0=ALU.mult, op1=ALU.mult), reads=[hqb, rsb, b_const], writes=[xnb[si]])

        def post_norm_resid(cx, seq, use_x, s, gcol, fT, fTb, last):
            sf = src_h(seq, use_x)
            df = dst_h(seq, last)
            hbq = hbufs[seq][s]
            for si, (col0, n) in enumerate(cfg.segs(s)):
                xo = 0 if si == 0 else 512
                bank, bb = cx.nbank.next()
                for q in range(4):
                    sq, sqb = cx.sq.next()
                    P.op("act", lambda e, sq=sq, q=q, n=n, xo=xo: e.activation(
                        out=sq[:, :, 0:n], in_=fT[:, 4 * q:4 * q + 4, xo:xo + n], func=AF.Square),
                        reads=[fTb[si]], writes=[sqb])
                    for c in range(4):
                        P.op("pe", lambda e, sq=sq, bank=bank, c=c, q=q, n=n: e.matmul(
                            bank[:, 0:n], lhsT=ones_bf[:, :], rhs=sq[:, c, 0:n], start=(q == 0 and c == 0),
                            stop=(q == 3 and c == 3)), reads=[sqb, b_const], writes=[bb])
                rs, rsb = cx.rstd.next()
                P.op("act", lambda e, rs=rs, bank=bank, n=n: e.activation(
                    out=rs[:, 0:n], in_=bank[:, 0:n], func=AF.Ln, scale=1.0 / D, bias=EPS), reads=[bb], writes=[rsb])
                P.op("act", lambda e, rs=rs, n=n: e.activation(out=rs[:, 0:n], in_=rs[:, 0:n], func=AF.Exp,
                                                               scale=-0.5), reads=[rsb], writes=[rsb])
                for q in range(4):
                    dst = df(col0, n, 4 * q, 4)
                    if dst is None:
                        continue
                    hq, hqb = cx.hq.next()
                    P.op("pool", lambda e, hq=hq, q=q, col0=col0, n=n: e.dma_start(
                        out=hq[:, :, 0:n], in_=sf(col0, n, 4 * q, 4)), reads=[hbq[q]], writes=[hqb], dma="L" + hqb.name)
                    for c in range(4):
                        cc = 4 * q + c
                        P.op("dve", lambda e, cc=cc, rs=rs, n=n, xo=xo: e.scalar_tensor_tensor(
                            out=fT[:, cc, xo:xo + n], in0=fT[:, cc, xo:xo + n],
                            scalar=gains[:, gcol + cc:gcol + cc + 1], in1=rs[:, 0:n], op0=ALU.mult, op1=ALU.mult),
                            reads=[fTb[si], rsb, b_const], writes=[fTb[si]])
                    P.op("pool", lambda e, hq=hq, q=q, n=n, xo=xo: e.tensor_tensor(
                        out=hq[:, :, 0:n], in0=hq[:, :, 0:n], in1=fT[:, 4 * q:4 * q + 4, xo:xo + n], op=ALU.add),
                        reads=[hqb, fTb[si]], writes=[hqb])
                    P.op("pool", lambda e, hq=hq, dst=dst, n=n: e.dma_start(out=dst, in_=hq[:, :, 0:n]),
                         reads=[hqb], writes=[hbq[q]], dma="S" + hqb.name)

        def proj2048(cx, m0, cnt, xin_fn, xin_bufs, segs, epi):
            g0 = 0
            while g0 < cnt:
                k = min(4, cnt - g0)
                wa, _, wb = cx.ring.next()
                P.op("sp", lambda e, wa=wa, g0=g0, k=k: e.dma_start(
                    out=wa[:, 0:k, :, :].rearrange("p m k c -> p m (k c)"),
                    in_=Wb[m0 + g0:m0 + g0 + k].rearrange("m p e -> p m e")),
                    reads=[b_wscr], writes=[wb], dma="L" + wb.name)
                for mi in range(k):
                    for si, (col0, n) in enumerate(segs):
                        bank, bb = cx.mbank.next()
                        for kc in range(KC):
                            P.op("pe", lambda e, wa=wa, mi=mi, kc=kc, bank=bank, si=si, n=n: e.matmul(
                                bank[:, 0:n], lhsT=wa[:, mi, kc, :], rhs=xin_fn(si, kc, n), start=(kc == 0),
                                stop=(kc == KC - 1)), reads=[wb, xin_bufs[si]], writes=[bb])
                        epi(g0 + mi, si, n, bank, bb)
                g0 += k

        evk = [0]

        def evac(out_fn, bank, bb, n, wbufs, rows=128):
            evk[0] += 1
            if evk[0] % 2 == 0:
                P.op("dve", lambda e: e.tensor_copy(out=out_fn(), in_=bank[0:rows, 0:n]), reads=[bb], writes=wbufs)
            else:
                P.op("act", lambda e: e.copy(out=out_fn(), in_=bank[0:rows, 0:n]), reads=[bb], writes=wbufs)

        def common_ctx(nring, nhq=2, nsq=2, nrstd=2):
            cx = Ctx()
            cx.ring = alloc_ring(nring) if nring else None
            cx.hq = Ring([(ar.alloc([128, 4, 528], F32), Buf("hq%d" % i)) for i in range(nhq)])
            cx.sq = Ring([(ar.alloc([128, 4, 528], BF16), Buf()) for _ in range(nsq)])
            cx.rstd = Ring([(ar.alloc([128, 512], F32), Buf()) for _ in range(nrstd)])
            cx.nbank = bankring([6, 7])
            cx.mbank = bankring([0, 1, 2, 3, 4, 5])
            return cx

        def stage_C(seq, layer):
            m = ar.mark()
            cx = common_ctx(3)
            fTs = [(ar.alloc([128, KC, 528], F32), [Buf(), Buf()]) for _ in range(2)]
            xin = [(ar.alloc([128, KC, 528], BF16), [Buf("xi%d_0" % i), Buf("xi%d_1" % i)]) for i in range(2)]
            m0, cnt = cfg.mch[(layer, "wo")]
            ov = hview(oscr[seq])
            for s in range(NST):
                segs = cfg.segs(s)
                xi, xib = xin[s % 2]
                fT, fTb = fTs[s % 2]
                for si, (col0, n) in enumerate(segs):
                    xo = 0 if si == 0 else 512
                    P.op("sp", lambda e, xi=xi, col0=col0, n=n, xo=xo: e.dma_start(
                        out=xi[:, :, xo:xo + n], in_=ov[:, :, col0:col0 + n]), reads=[obufs[seq][s]],
                        writes=[xib[si]], dma="L" + xib[si].name)

                def xin_fn(si, kc, n, xi=xi):
                    xo = 0 if si == 0 else 512
                    return xi[:, kc, xo:xo + n]

                def epi(mi, si, n, bank, bb, fTb=fTb, fT=fT):
                    xo = 0 if si == 0 else 512
                    evac(lambda: fT[:, mi, xo:xo + n], bank, bb, n, [fTb[si]])

                proj2048(cx, m0, cnt, xin_fn, xib, segs, epi)
                post_norm_resid(cx, seq, layer == 0, s, gain_col(cfg, layer, "mix_post"), fT, fTb, False)
            P.barrier()
            ar.release(m)

        def stage_D(seq, layer):
            m = ar.mark()
            cx = common_ctx(3, nsq=4, nrstd=1)
            fT = ar.alloc([128, KC, 528], F32)
            xn = ar.alloc([128, KC, 528], BF16)
            hff = ar.alloc([128, 64, 528], BF16)
            rr = Ring([(ar.alloc([128, 512], F32), Buf()) for _ in range(2)])
            m0, cnt = cfg.mch[(layer, "win")]
            last = (layer == cfg.depth - 1)
            xnb = [Buf(), Buf()]
            hfb = [Buf(), Buf()]
            fTb = [Buf(), Buf()]
            load_norm(cx, seq, False, 0, gain_col(cfg, layer, "mlp_pre"), xn, xnb)
            for s in range(NST):
                segs = cfg.segs(s)

                def xin_fn(si, kc, n):
                    xo = 0 if si == 0 else 512
                    return xn[:, kc, xo:xo + n]

                def epi(mi, si, n, bank, bb, hfb=hfb):
                    xo = 0 if si == 0 else 512
                    r, rb = rr.next()
                    P.op("act", lambda e: e.activation(out=r[:, 0:n], in_=bank[:, 0:n], func=AF.Relu),
                         reads=[bb], writes=[rb])
                    P.op("pool", lambda e: e.tensor_tensor(out=hff[:, mi, xo:xo + n], in0=r[:, 0:n], in1=r[:, 0:n],
                                                          op=ALU.mult), reads=[rb], writes=[hfb[si]])

                proj2048(cx, m0, cnt, xin_fn, xnb, segs, epi)
                for dmc in range(16):
                    _, wbt, wb = cx.ring.next()
                    P.op("sp", lambda e, wbt=wbt, dmc=dmc: e.dma_start(
                        out=wbt[:, :, :].rearrange("p f c -> p (f c)"), in_=Woutb[layer * 16 + dmc]),
                        reads=[b_wscr], writes=[wb], dma="L" + wb.name)
                    if s + 1 < NST and len(cfg.segs(s + 1)) == 1:
                        if dmc == 2:
                            ln_st = load_norm_a(cx, seq, False, s + 1)
                        if dmc == 7:
                            load_norm_b(cx, seq, False, s + 1, gain_col(cfg, layer, "mlp_pre"), xn, xnb, ln_st)
                    elif s + 1 < NST and dmc == 3:
                        load_norm(cx, seq, False, s + 1, gain_col(cfg, layer, "mlp_pre"), xn, xnb)
                    for si, (col0, n) in enumerate(segs):
                        xo = 0 if si == 0 else 512
                        bank, bb = cx.mbank.next()
                        for fc in range(64):
                            P.op("pe", lambda e, wbt=wbt, fc=fc, bank=bank, n=n, xo=xo: e.matmul(
                                bank[:, 0:n], lhsT=wbt[:, fc, :], rhs=hff[:, fc, xo:xo + n], start=(fc == 0),
                                stop=(fc == 63)), reads=[wb, hfb[si]], writes=[bb])
                        evac(lambda dmc=dmc, xo=xo, n=n: fT[:, dmc, xo:xo + n], bank, bb, n, [fTb[si]])
                post_norm_resid(cx, seq, False, s, gain_col(cfg, layer, "mlp_post"), fT, fTb, last)
            P.barrier()
            ar.release(m)

        def stage_MLA(seq, layer):
            j = layer // 2
            m = ar.mark()
            cqT = ar.alloc([128, 4, L], BF16)
            ckvT = ar.alloc([128, 4, L], BF16)
            krT = ar.alloc([128, L], BF16)
            cs = ar.alloc([128, L], F32)
            b_cq = [Buf() for _ in range(NST)]
            b_ckv = [Buf() for _ in range(NST)]
            b_kr = [Buf() for _ in range(NST)]
            b_cs = Buf()
            P.op("sp", lambda e: e.dma_start(out=cs[:], in_=cs_d[:, :]), writes=[b_cs], dma="cs")
            P.op("pool", lambda e: e.memset(krT[64:128, :], 0.0), writes=b_kr)
            mA = ar.mark()
            cx = common_ctx(3)
            xn = ar.alloc([128, KC, 528], BF16)
            pre = Ring([(ar.alloc([128, 4, 528], F32), [Buf(), Buf()]) for _ in range(2)])
            prod = ar.alloc([128, 528], F32)
            tmp = ar.alloc([64, 528], F32)
            b_prod = Buf()
            b_tmp = Buf()
            mdq, _ = cfg.mch[(layer, "dq")]
            mdkv, _ = cfg.mch[(layer, "dkv")]
            xnb = [Buf(), Buf()]
            for s in range(NST):
                segs = cfg.segs(s)
                load_norm(cx, seq, layer == 0, s, gain_col(cfg, layer, "mix_pre"), xn, xnb)

                def xin_fn(si, kc, n):
                    xo = 0 if si == 0 else 512
                    return xn[:, kc, xo:xo + n]

                for which in ("q", "kv"):
                    pr, prb = pre.next()

                    def epi(mi, si, n, bank, bb, pr=pr, prb=prb, s=s):
                        xo = 0 if si == 0 else 512
                        col0 = segs[si][0]
                        if mi < 4:
                            evac(lambda: pr[:, mi, xo:xo + n], bank, bb, n, [prb[si]])
                        else:
                            P.op("dve", lambda e: e.tensor_tensor(out=prod[:, 0:n], in0=bank[:, 0:n],
                                                                  in1=cs[:, col0:col0 + n], op=ALU.mult),
                                 reads=[bb, b_cs], writes=[b_prod])
                            P.op("dve", lambda e: e.tensor_copy(out=tmp[0:64, 0:n], in_=prod[64:128, 0:n]),
                                 reads=[b_prod], writes=[b_tmp])
                            P.op("dve", lambda e: e.tensor_tensor(out=krT[0:64, col0:col0 + n], in0=prod[0:64, 0:n],
                                                                  in1=tmp[0:64, 0:n], op=ALU.add),
                                 reads=[b_prod, b_tmp], writes=[b_kr[s]])

                    if which == "q":
                        proj2048(cx, mdq, 4, xin_fn, xnb, segs, epi)
                        dstT, dstb, gc = cqT, b_cq, lat_gain_col(cfg, j, "q")
                    else:
                        proj2048(cx, mdkv, 5, xin_fn, xnb, segs, epi)
                        dstT, dstb, gc = ckvT, b_ckv, lat_gain_col(cfg, j, "kv")
                    for si, (col0, n) in enumerate(segs):
                        xo = 0 if si == 0 else 512
                        srcs = [((lambda c=c, xo=xo, n=n, pr=pr: pr[:, c, xo:xo + n]), prb[si]) for c in range(4)]
                        rs, rsb = norm_stats(cx, srcs, n, 4, 1.0 / 512)
                        for c in range(4):
                            P.op("dve", lambda e, c=c, xo=xo, n=n, col0=col0, pr=pr, rs=rs, dstT=dstT, gc=gc:
                                 e.scalar_tensor_tensor(out=dstT[:, c, col0:col0 + n], in0=pr[:, c, xo:xo + n],
                                                        scalar=gains[:, gc + c:gc + c + 1], in1=rs[:, 0:n],
                                                        op0=ALU.mult, op1=ALU.mult),
                                 reads=[prb[si], rsb, b_const], writes=[dstb[s]])
            P.barrier()
            ar.release(mA)
            KT = cfg.ktiles()
            NKT = len(KT)
            knT = Ring([(ar.alloc([128, L], BF16), Buf()) for _ in range(2)])
            vh = Ring([(ar.alloc([128, NKT, 128], BF16), Buf()) for _ in range(2)])
            wq = Ring([(ar.alloc([128, 4, 256], BF16), Buf("wq%d" % i)) for i in range(2)])
            wkv = Ring([(ar.alloc([128, 4, 256], BF16), Buf("wk%d" % i)) for i in range(2)])
            qn = Ring([(ar.alloc([128, 512], BF16), Buf()) for _ in range(2)])
            qr = Ring([(ar.alloc([128, 512], BF16), Buf()) for _ in range(2)])
            for (qrt_, qrb_) in qr.items:
                P.op("pool", lambda e, qrt_=qrt_: e.memset(qrt_[64:128, :], 0.0), writes=[qrb_])
            acc = Ring([(ar.alloc([128, 512], F32), Buf()) for _ in range(2)])
            prod = ar.alloc([128, 512], F32)
            tmp = ar.alloc([64, 512], F32)
            rden = ar.alloc([128, 512], F32)
            ob = Ring([(ar.alloc([128, 512], BF16), Buf("ob%d" % i)) for i in range(2)])
            b_prod, b_tmp, b_rden = Buf(), Buf(), Buf()
            sbank = bankring([0, 1, 2])
            obank = bankring([3, 4])
            dbank = bankring([5, 6])
            xbank = bankring([7])
            scale = (128 + 64) ** -0.5
            ovh = oscr[seq]
            allcq = b_cq
            allckv = b_ckv
            acck = [0]
            LA = 3
            pT = Ring([(ar.alloc([128, 512], BF16), Buf()) for _ in range(8)])

            def head_setup(h):
                wqt, wqb = wq.next()
                wkt, wkb = wkv.next()
                P.op("sp", lambda e: e.dma_start(
                    out=wqt[:, :, :].rearrange("p k c -> p (k c)"), in_=Wuqb[j * 16 + h]), reads=[b_wscr],
                    writes=[wqb], dma="L" + wqb.name)
                P.op("sp", lambda e: e.dma_start(
                    out=wkt[:, :, :].rearrange("p k c -> p (k c)"), in_=Wukvb[j * 16 + h]), reads=[b_wscr],
                    writes=[wkb], dma="L" + wkb.name)
                kn, knb = knT.next()
                vt, vtb = vh.next()
                for s in range(NST):
                    for (col0, n) in cfg.segs(s):
                        bank, bb = xbank.next()
                        for kc in range(4):
                            P.op("pe", lambda e, kc=kc, bank=bank, col0=col0, n=n: e.matmul(
                                bank[:, 0:n], lhsT=wkt[:, kc, 0:128], rhs=ckvT[:, kc, col0:col0 + n],
                                start=(kc == 0), stop=(kc == 3)), reads=[wkb, allckv[s]], writes=[bb])
                        P.op("dve", lambda e, bank=bank, col0=col0, n=n: e.tensor_copy(
                            out=kn[:, col0:col0 + n], in_=bank[:, 0:n]), reads=[bb], writes=[knb])
                for t0 in range(0, NKT, 4):
                    bank, bb = xbank.next()
                    tl = KT[t0:t0 + 4]
                    for ti, (col0, n) in enumerate(tl):
                        s = min(col0 // 512, NST - 1)
                        for kc in range(4):
                            P.op("pe", lambda e, kc=kc, bank=bank, col0=col0, n=n, ti=ti: e.matmul(
                                bank[0:n, ti * 128:(ti + 1) * 128], lhsT=ckvT[:, kc, col0:col0 + n],
                                rhs=wkt[:, kc, 128:256], start=(kc == 0), stop=(kc == 3)),
                                reads=[wkb, allckv[s]], writes=[bb])
                    for ti, (col0, n) in enumerate(tl):
                        P.op("dve", lambda e, bank=bank, t=t0 + ti, ti=ti, n=n: e.tensor_copy(
                            out=vt[0:n, t, :], in_=bank[0:n, ti * 128:(ti + 1) * 128]), reads=[bb], writes=[vtb])
                return dict(wqt=wqt, wqb=wqb, kn=kn, knb=knb, vt=vt, vtb=vtb)

            def build_q(item, H):
                h, s, q0, nq = item
                wqt, wqb = H["wqt"], H["wqb"]
                qnt, qnb = qn.next()
                qrt, qrb = qr.next()
                bank, bb = xbank.next()
                for kc in range(4):
                    P.op("pe", lambda e, kc=kc: e.matmul(
                        bank[:, 0:nq], lhsT=wqt[:, kc, 0:128], rhs=cqT[:, kc, q0:q0 + nq],
                        start=(kc == 0), stop=(kc == 3)), reads=[wqb, allcq[s]], writes=[bb])
                P.op("dve", lambda e: e.tensor_copy(out=qnt[:, 0:nq], in_=bank[:, 0:nq]), reads=[bb], writes=[qnb])
                bank2, bb2 = xbank.next()
                for kc in range(4):
                    P.op("pe", lambda e, kc=kc: e.matmul(
                        bank2[:, 0:nq], lhsT=wqt[:, kc, 128:256], rhs=cqT[:, kc, q0:q0 + nq],
                        start=(kc == 0), stop=(kc == 3)), reads=[wqb, allcq[s]], writes=[bb2])
                P.op("dve", lambda e: e.tensor_tensor(
                    out=prod[:, 0:nq], in0=bank2[:, 0:nq], in1=cs[:, q0:q0 + nq], op=ALU.mult),
                    reads=[bb2, b_cs], writes=[b_prod])
                P.op("dve", lambda e: e.tensor_copy(out=tmp[0:64, 0:nq], in_=prod[64:128, 0:nq]),
                     reads=[b_prod], writes=[b_tmp])
                P.op("dve", lambda e: e.tensor_tensor(
                    out=qrt[0:64, 0:nq], in0=prod[0:64, 0:nq], in1=tmp[0:64, 0:nq], op=ALU.add),
                    reads=[b_prod, b_tmp], writes=[qrb])
                return (qnt, qnb, qrt, qrb)

            def make_finish(item, po, pob, ac, acb, pd, pdb):
                h, s, q0, nq = item

                def fin():
                    bank3, bb3 = pd, pdb
                    P.op("pe", lambda e: e.matmul(
                        bank3[:, 0:nq], lhsT=ones_f[:, :], rhs=ac[:, 0:nq], start=False, stop=True),
                        reads=[acb, b_const], writes=[bb3])
                    P.op("act", lambda e: e.activation(out=rden[:, 0:nq], in_=bank3[:, 0:nq], func=AF.Ln),
                         reads=[bb3], writes=[b_rden])
                    P.op("act", lambda e: e.activation(out=rden[:, 0:nq], in_=rden[:, 0:nq], func=AF.Exp, scale=-1.0),
                         reads=[b_rden], writes=[b_rden])
                    obt, obb = ob.next()
                    P.op("dve", lambda e: e.tensor_tensor(
                        out=obt[:, 0:nq], in0=po[:, 0:nq], in1=rden[:, 0:nq], op=ALU.mult),
                        reads=[pob, b_rden], writes=[obb])
                    P.op("pool", lambda e: e.dma_start(
                        out=ovh[h * 128:(h + 1) * 128, q0:q0 + nq], in_=obt[:, 0:nq]), reads=[obb],
                        writes=[obufs[seq][s]], dma="S" + obb.name)
                return fin

            items = []
            for h in range(16):
                for s in range(NST):
                    for (q0, nq) in cfg.segs(s):
                        items.append((h, s, q0, nq))
            hd = {0: head_setup(0)}
            qctx = {0: build_q(items[0], hd[0])}
            prev_fin = None
            if pp_late:
                ppf, ppo = pp_alloc()
                pp_per_item = -(-len(pp_late) // max(1, len(items) - 8))
            for i, item in enumerate(items):
                if pp_late and i >= 2:
                    for _ in range(pp_per_item):
                        if pp_late:
                            pp_emit(pp_late.pop(0), ppf, ppo, "pool")
                h, s, q0, nq = item
                H = hd[h]
                kn, knb, vt, vtb = H["kn"], H["knb"], H["vt"], H["vtb"]
                qnt, qnb, qrt, qrb = qctx.pop(i)
                po, pob = obank.next()
                pd, pdb = dbank.next()
                ac, acb = acc.next()
                sts = {}

                def emitST(t, kn=kn, knb=knb, qnt=qnt, qnb=qnb, qrt=qrt, qrb=qrb, nq=nq, sts=sts):
                    k0, nk = KT[t]
                    sk = min(k0 // 512, NST - 1)
                    sb_, sbb = sbank.next()
                    P.op("pe", lambda e: e.matmul(
                        sb_[0:nk, 0:nq], lhsT=kn[:, k0:k0 + nk], rhs=qnt[:, 0:nq], start=True, stop=False),
                        reads=[knb, qnb], writes=[sbb])
                    P.op("pe", lambda e: e.matmul(
                        sb_[0:nk, 0:nq], lhsT=krT[:, k0:k0 + nk], rhs=qrt[:, 0:nq], start=False,
                        stop=True), reads=[b_kr[sk], qrb], writes=[sbb])
                    sts[t] = (sb_, sbb)

                for t in range(min(LA, NKT)):
                    emitST(t)
                if i + 1 < len(items):
                    h2 = items[i + 1][0]
                    if h2 != h:
                        hd[h2] = head_setup(h2)
                    qctx[i + 1] = build_q(items[i + 1], hd[h2])
                if prev_fin is not None:
                    prev_fin()
                    prev_fin = None
                for t, (k0, nk) in enumerate(KT):
                    sb_, sbb = sts.pop(t)
                    pt, ptb = pT.next()
                    P.op("act", lambda e, pt=pt, sb_=sb_, nk=nk, nq=nq: e.activation(
                        out=pt[0:nk, 0:nq], in_=sb_[0:nk, 0:nq], func=AF.Exp, scale=scale),
                        reads=[sbb], writes=[ptb])
                    if t + LA < NKT:
                        emitST(t + LA)
                    P.op("pe", lambda e, po=po, vt=vt, pt=pt, t=t, nk=nk, nq=nq: e.matmul(
                        po[:, 0:nq], lhsT=vt[0:nk, t, :], rhs=pt[0:nk, 0:nq], start=(t == 0),
                        stop=(t == NKT - 1)), reads=[vtb, ptb], writes=[pob])
                    if t % 2 == 0:
                        if t == 0:
                            P.op("dve", lambda e, ac=ac, pt=pt, nq=nq: e.tensor_copy(out=ac[:, 0:nq], in_=pt[:, 0:nq]),
                                 reads=[ptb], writes=[acb])
                        else:
                            P.op("dve", lambda e, ac=ac, pt=pt, nk=nk, nq=nq: e.tensor_tensor(
                                out=ac[0:nk, 0:nq], in0=ac[0:nk, 0:nq], in1=pt[0:nk, 0:nq], op=ALU.add),
                                reads=[ptb, acb], writes=[acb])
                    else:
                        P.op("pe", lambda e, pd=pd, pt=pt, t=t, nk=nk, nq=nq: e.matmul(
                            pd[:, 0:nq], lhsT=ones_bf[0:nk, :], rhs=pt[0:nk, 0:nq], start=(t == 1), stop=False),
                            reads=[ptb, b_const], writes=[pdb])
                prev_fin = make_finish(item, po, pob, ac, acb, pd, pdb)
                if h > 0 and (i + 1 == len(items) or items[i + 1][0] != h):
                    hd.pop(h - 1, None)
            if prev_fin is not None:
                prev_fin()
            while pp_late:
                pp_emit(pp_late.pop(0), ppf, ppo, "pool")
            P.barrier()
            ar.release(m)

        def build_G():
            m = ar.mark()
            rb = ar.alloc([33, 32], F32)
            oh = ar.alloc([33, OH_N], F32)
            gsb = ar.alloc([32, OH_N], BF16)
            b_rb, b_oh, b_g = Buf(), Buf(), Buf()
            P.op("pool", lambda e: e.memset(rb[32:33, :], -30000.0), writes=[b_rb])
            P.op("sp", lambda e: e.dma_start(out=rb[0:32, :], in_=relb_d[:, :]), writes=[b_rb], dma="g1")
            P.op("sp", lambda e: e.dma_start(out=oh[:, :], in_=oh_d[:, :]), writes=[b_oh], dma="g2")
            for (c0, n) in ((0, 512), (512, OH_N - 512)):
                bank, bb = PS[0], PSB[0]
                P.op("pe", lambda e, c0=c0, n=n: e.matmul(bank[0:32, 0:n], lhsT=rb[0:33, 0:32], rhs=oh[0:33, c0:c0 + n],
                                                          start=True, stop=True), reads=[b_rb, b_oh], writes=[bb])
                P.op("act", lambda e, c0=c0, n=n: e.activation(out=gsb[0:32, c0:c0 + n], in_=bank[0:32, 0:n],
                                                               func=AF.Exp), reads=[bb], writes=[b_g])
            b_gs = Buf()
            P.op("pool", lambda e: e.dma_start(out=Gs[:, :], in_=gsb[0:32, :]), reads=[b_g], writes=[b_gs], dma="g3")
            tR = ar.alloc([128, 32, 128], BF16)
            tF = ar.alloc([128, 32, 128], BF16)
            b_tR, b_tF = Buf(), Buf()

            def mk(dst, off, npart, nq):
                P.op("sp", lambda e: e.dma_start(out=tR[0:npart, :, 0:nq],
                                                 in_=AP(Gs.tensor, off, [[1, npart], [OH_N, 32], [1, nq]])),
                     reads=[b_gs], writes=[b_tR], dma="g4")
                P.op("dve", lambda e: e.tensor_copy(out=tF[0:npart, :, 0:nq],
                                                    in_=tR[0:npart, :, slice(nq - 1, None, -1)]),
                     reads=[b_tR], writes=[b_tF])
                P.op("pool", lambda e: e.dma_start(out=dst, in_=tF[0:npart, :, 0:nq]), reads=[b_tF], dma="g5")

            for d_ in range(3):
                mk(EBm_d[:, d_, :, :], OH_MAIN + 128 * d_, 128, 128)
            mk(EBk0_d[:, :, :], OH_K0, 16, 128)
            mk(EBkc_d[:, :, :], OH_KC, 16, 128)
            mk(EBq4_d[:, :, :], OH_Q4, 16, 16)
            mk(EBq5_d[:, :, :], OH_Q5, 128, 16)
            P.barrier()
            ar.release(m)

        def stage_SWA(seq, layer):
            j = layer // 2
            m = ar.mark()
            KT = cfg.ktiles()
            NKT = len(KT)
            kT = ar.alloc([128, 2, L], BF16)
            vx = ar.alloc([128, NKT, 4, 128], BF16)
            b_k = [Buf() for _ in range(NST)]
            b_v = [Buf() for _ in range(NST)]
            b_vones = Buf()
            P.op("pool", lambda e: e.memset(vx[:], 1.0), writes=[b_vones] + b_v)
            mkv, _ = cfg.mch[(layer, "kv")]
            mq, _ = cfg.mch[(layer, "q")]
            m1 = ar.mark()
            cx = common_ctx(2)
            xn = ar.alloc([128, KC, 528], BF16)
            xnb = [Buf(), Buf()]
            for s in range(NST):
                segs = cfg.segs(s)
                load_norm(cx, seq, layer == 0, s, gain_col(cfg, layer, "mix_pre"), xn, xnb)
                wa, _, wb = cx.ring.next()
                P.op("sp", lambda e, wa=wa: e.dma_start(
                    out=wa[:, 0:4, :, :].rearrange("p m k c -> p m (k c)"),
                    in_=Wb[mkv:mkv + 4].rearrange("m p e -> p m e")), reads=[b_wscr], writes=[wb], dma="L" + wb.name)
                for si, (col0, n) in enumerate(segs):
                    xo = 0 if si == 0 else 512
                    for pr_ in range(2):
                        bank, bb = cx.mbank.next()
                        for kc in range(KC):
                            P.op("pe", lambda e, wa=wa, pr_=pr_, kc=kc, bank=bank, xo=xo, n=n: e.matmul(
                                bank[:, 0:n], lhsT=wa[:, pr_, kc, :], rhs=xn[:, kc, xo:xo + n], start=(kc == 0),
                                stop=(kc == KC - 1)), reads=[wb, xnb[si]], writes=[bb])
                        evac(lambda pr_=pr_, col0=col0, n=n: kT[:, pr_, col0:col0 + n], bank, bb, n, [b_k[s]])
                    for t0 in range(0, n, 128):
                        nt = min(128, n - t0)
                        kt = (col0 + t0) // 128
                        bank, bb = cx.mbank.next()
                        for kc in range(KC):
                            P.op("pe", lambda e, wa=wa, kc=kc, bank=bank, xo=xo, t0=t0, nt=nt: e.matmul(
                                bank[0:nt, 0:256], lhsT=xn[:, kc, xo + t0:xo + t0 + nt], rhs=wa[:, 2:4, kc, :],
                                start=(kc == 0), stop=(kc == KC - 1)), reads=[wb, xnb[si]], writes=[bb])
                        P.op("dve", lambda e, bank=bank, kt=kt, nt=nt: e.tensor_copy(
                            out=vx[0:nt, kt, :, 0:64], in_=bank[0:nt, 0:256].rearrange("p (g d) -> p g d", g=4)),
                            reads=[bb], writes=[b_v[s]])
            P.barrier()
            ar.release(m1)
            cx = common_ctx(2)
            xo_buf = ar.alloc([128, KC, 528], BF16)
            qT = ar.alloc([128, KC, 528], BF16)
            EBm = ar.alloc([128, 3, 32, 128], BF16)
            EBk0 = ar.alloc([16, 32, 128], BF16)
            EBkc = ar.alloc([16, 32, 128], BF16)
            EBq4 = ar.alloc([16, 32, 16], BF16)
            EBq5 = ar.alloc([128, 32, 16], BF16)
            es32 = ar.alloc([1, 32], F32)
            esb = ar.alloc([1, 32], BF16)
            vsink = ar.alloc([1, 128], BF16)
            pT = Ring([(ar.alloc([128, 512], BF16), Buf()) for _ in range(4)])
            pT2 = Ring([(ar.alloc([128, 512], BF16), Buf()) for _ in range(8)])
            rden = Ring([(ar.alloc([64, 512], F32), Buf()) for _ in range(3)])
            b_eb, b_es = Buf(), Buf()
            P.op("sp", lambda e: e.dma_start(out=EBm[:, :, :, :], in_=EBm_d[:, :, :, :]), writes=[b_eb], dma="eb")
            P.op("sp", lambda e: e.dma_start(out=EBk0[:, :, :], in_=EBk0_d[:, :, :]), writes=[b_eb], dma="eb")
            P.op("sp", lambda e: e.dma_start(out=EBkc[:, :, :], in_=EBkc_d[:, :, :]), writes=[b_eb], dma="eb")
            P.op("sp", lambda e: e.dma_start(out=EBq4[:, :, :], in_=EBq4_d[:, :, :]), writes=[b_eb], dma="eb")
            P.op("sp", lambda e: e.dma_start(out=EBq5[:, :, :], in_=EBq5_d[:, :, :]), writes=[b_eb], dma="eb")
            P.op("sp", lambda e: e.dma_start(out=es32[0:1, :], in_=sink_d[j:j + 1, :]), writes=[b_es], dma="es")
            P.op("act", lambda e: e.activation(out=esb[0:1, :], in_=es32[0:1, :], func=AF.Exp), reads=[b_es],
                 writes=[b_es])
            P.op("pool", lambda e: e.memset(vsink[0:1, 0:64], 0.0), writes=[b_es])
            P.op("pool", lambda e: e.memset(vsink[0:1, 64:128], 1.0), writes=[b_es])
            sbank = bankring([0, 1, 2, 3, 4, 5])
            obank = bankring([6, 7])
            ebk = [0]
            ov = hview(oscr[seq])

            def esink_ap(h0, nq):
                a = esb[0:1, h0:h0 + 4]
                return AP(a.tensor, a.offset, [list(a.ap[0]), [1, 4], [0, nq]])

            class Unit:
                def __init__(u, s, xob, qb, qcol, nq, g, j0, keys):
                    u.s, u.xob, u.qb, u.qcol, u.nq, u.g, u.j0, u.keys = s, xob, qb, qcol, nq, g, j0, keys
                    u.p_, u.half = divmod(g, 2)
                    u.hb0 = u.half * 64
                    u.c0 = u.p_ * 8 + j0
                    u.h0 = 8 * g + j0
                    u.N = 4 * nq
                    u.sts = []

                def A(u):
                    hb0, p_, c0, qcol, nq, N = u.hb0, u.p_, u.c0, u.qcol, u.nq, u.N
                    for ki, (kt, nk, ebf) in enumerate(u.keys):
                        k0 = KT[kt][0]
                        sk = min(k0 // 512, NST - 1)
                        sb_, sbb = sbank.next()
                        P.op("pe", lambda e, sb_=sb_, k0=k0, nk=nk: e.matmul(
                            sb_[0:nk, 0:N], lhsT=kT[hb0:hb0 + 64, p_, k0:k0 + nk],
                            rhs=qT[hb0:hb0 + 64, c0:c0 + 4, qcol:qcol + nq], start=True, stop=True),
                            reads=[b_k[sk], u.qb], writes=[sbb])
                        u.sts.append((sb_, sbb))

                def B(u):
                    N, h0 = u.N, u.h0
                    u.p2 = []
                    for ki, (kt, nk, ebf) in enumerate(u.keys):
                        sb_, sbb = u.sts[ki]
                        pt, ptb = pT.next()
                        P.op("act", lambda e, pt=pt, sb_=sb_, nk=nk: e.activation(
                            out=pt[0:nk, 0:N], in_=sb_[0:nk, 0:N], func=AF.Exp, scale=0.125), reads=[sbb],
                            writes=[ptb])
                        pt2, pt2b = pT2.next()
                        ebk[0] += 1
                        eng = "pool" if ebk[0] % 6 == 0 else "dve"
                        P.op(eng, lambda e, pt=pt, pt2=pt2, nk=nk, ebf=ebf: e.tensor_tensor(
                            out=pt2[0:nk, 0:N], in0=pt[0:nk, 0:N], in1=ebf(h0).rearrange("p h q -> p (h q)"),
                            op=ALU.mult), reads=[ptb, b_eb], writes=[pt2b])
                        u.p2.append((pt2, pt2b))

                def C(u):
                    N, h0, g, nq = u.N, u.h0, u.g, u.nq
                    po, pob = obank.next()
                    u.po, u.pob = po, pob
                    order = sorted(range(len(u.keys)), key=lambda ki: (u.keys[ki][1] < 128))
                    for oi, ki in enumerate(order):
                        kt, nk, ebf = u.keys[ki]
                        k0 = KT[kt][0]
                        sk = min(k0 // 512, NST - 1)
                        pt2, pt2b = u.p2[ki]
                        P.op("pe", lambda e, pt2=pt2, kt=kt, nk=nk, oi=oi: e.matmul(
                            po[:, 0:N], lhsT=vx[0:nk, kt, g, :], rhs=pt2[0:nk, 0:N], start=(oi == 0), stop=False),
                            reads=[b_v[sk], b_vones, pt2b], writes=[pob])
                    P.op("pe", lambda e: e.matmul(po[:, 0:N], lhsT=vsink[0:1, :], rhs=esink_ap(h0, nq),
                                                  start=False, stop=True), reads=[b_es], writes=[pob])

                def Dk(u):
                    N, hb0, c0, qcol, nq = u.N, u.hb0, u.c0, u.qcol, u.nq
                    po, pob = u.po, u.pob
                    rd, rdb = rden.next()
                    P.op("act", lambda e: e.activation(out=rd[0:64, 0:N], in_=po[64:128, 0:N], func=AF.Ln),
                         reads=[pob], writes=[rdb])
                    P.op("act", lambda e: e.activation(out=rd[0:64, 0:N], in_=rd[0:64, 0:N], func=AF.Exp, scale=-1.0),
                         reads=[rdb], writes=[rdb])
                    P.op("dve", lambda e: e.tensor_tensor(
                        out=xo_buf[hb0:hb0 + 64, c0:c0 + 4, qcol:qcol + nq],
                        in0=po[0:64, 0:N].rearrange("p (h q) -> p h q", h=4),
                        in1=rd[0:64, 0:N].rearrange("p (h q) -> p h q", h=4), op=ALU.mult),
                        reads=[pob, rdb], writes=[u.xob])

            def run_units(units):
                n = len(units)
                if n == 0:
                    return
                units[0].A()
                for i in range(n):
                    units[i].B()
                    if i + 1 < n:
                        units[i + 1].A()
                    units[i].C()
                    if i > 0:
                        units[i - 1].Dk()
                units[n - 1].Dk()

            xnb = [Buf(), Buf()]
            qb = [Buf(), Buf()]
            for s in range(NST):
                segs = cfg.segs(s)
                load_norm(cx, seq, False, s, gain_col(cfg, layer, "mix_pre"), xo_buf, xnb)

                def xin_fn(si, kc, n):
                    xo = 0 if si == 0 else 512
                    return xo_buf[:, kc, xo:xo + n]

                def epi(mi, si, n, bank, bb, qb=qb):
                    xo = 0 if si == 0 else 512
                    evac(lambda: qT[:, mi, xo:xo + n], bank, bb, n, [qb[si]])

                proj2048(cx, mq, 16, xin_fn, xnb, segs, epi)
                units = []
                for bi in range(4):
                    b = 4 * s + bi
                    for g in range(4):
                        for j0 in (0, 4):
                            keys = []
                            keys.append((NKT - 1, 16, (lambda h0, b=b: (EBk0 if b == 0 else EBkc)[0:16, h0:h0 + 4, :])))
                            if b - 1 >= 0:
                                keys.append((b - 1, 128, (lambda h0: EBm[:, 0, h0:h0 + 4, :])))
                            keys.append((b, 128, (lambda h0: EBm[:, 1, h0:h0 + 4, :])))
                            if b + 1 < NB:
                                keys.append((b + 1, 128, (lambda h0: EBm[:, 2, h0:h0 + 4, :])))
                            units.append(Unit(s, xnb[0], qb[0], 128 * bi, 128, g, j0, keys))
                if len(segs) > 1:
                    for g in range(4):
                        for j0 in (0, 4):
                            keys = [(NKT - 1, 16, (lambda h0: EBq4[0:16, h0:h0 + 4, :])),
                                    (0, 128, (lambda h0: EBq5[:, h0:h0 + 4, :]))]
                            units.append(Unit(s, xnb[1], qb[1], 512, 16, g, j0, keys))
                run_units(units)
                for si, (col0, n) in enumerate(segs):
                    xo = 0 if si == 0 else 512
                    P.op("pool", lambda e, col0=col0, n=n, xo=xo: e.dma_start(
                        out=ov[:, :, col0:col0 + n], in_=xo_buf[:, :, xo:xo + n]), reads=[xnb[si]],
                        writes=[obufs[seq][s]], dma="Sxo%d" % si)
            P.barrier()
            ar.release(m)

        prepass()
        if cfg.nswa > 0:
            build_G()
        for seq in range(NSEQ):
            for layer in range(cfg.depth):
                if layer % 2 == 0:
                    stage_MLA(seq, layer)
                else:
                    stage_SWA(seq, layer)
                stage_C(seq, layer)
                stage_D(seq, layer)
        P.finalize()
        P.emit(nc)
    return nc, P


def run_cfg(cfg, xs_per_core, w, core_ids):
    W, wout, wuq, wukv, gains = prep_weights(cfg, w)
    nc, P = build_program(cfg)
    shared = {
        "metaT": np.ascontiguousarray(np.asarray(w["meta_tokens"], np.float32).T),
        "W2048": W.reshape(cfg.NM, 128, KC * 128),
        "Wout": wout.reshape(cfg.depth * 16, 128, 64 * 128),
        "Wuq": wuq.reshape(cfg.nmla * 16, 128, 1024),
        "Wukv": wukv.reshape(cfg.nmla * 16, 128, 1024),
        "gains": gains,
        "rel_bias": np.ascontiguousarray(np.asarray(w["rel_bias"], np.float32)),
        "sink": np.ascontiguousarray(np.asarray(w["swa_sink"], np.float32)[:max(cfg.nswa, 1)]),
        "cs_tab": build_cs(cfg),
        "onehot": build_onehot(),
    }
    in_maps = []
    for xs in xs_per_core:
        d = dict(shared)
        d["xT"] = np.ascontiguousarray(np.stack([np.asarray(x, np.float32).T for x in xs]))
        in_maps.append(d)
    res = run_bass_kernel_spmd(nc, in_maps, core_ids=core_ids)
    return res


def kernel(**inputs):
    cfg = Cfg(S=4096, depth=4, nseq=2)
    xp = np.asarray(inputs["x_prompt"], np.float32)
    xs = np.asarray(inputs["x_sample"], np.float32)
    seqs = [("p", i) for i in range(4)] + [("s", i) for i in range(8)]

    def get(t):
        return xp[t[1]] if t[0] == "p" else xs[t[1]]

    slot_seq = seqs + [("s", 4), ("s", 5), ("s", 6), ("s", 7)]
    per_core = [[get(slot_seq[c]), get(slot_seq[8 + c])] for c in range(NCORES)]
    w = {k: np.asarray(v) for k, v in inputs.items() if k not in ("x_prompt", "x_sample")}
    res = run_cfg(cfg, per_core, w, list(range(NCORES)))
    yp = np.empty_like(xp)
    ys = np.empty_like(xs)
    for c in range(NCORES):
        y = res.results[c]["yT"]
        for q, slot in enumerate((c, 8 + c)):
            if slot >= 12:
                continue
            t = slot_seq[slot]
            (yp if t[0] == "p" else ys)[t[1]] = y[q].T
    return (yp, ys)
```
